# Optimizing a Trainium2 kernel written in Bass

```python
import jax, jax.numpy as jnp
from jax import lax
import numpy as np

D_MODEL = 2048
BATCH = 4
SEQ = 2048
DEPTH = 2
DEC_BATCH = 8
DEC_SEQ = 4
PAST_LEN = 16384
PAGE_SIZE = 128

BR_W = D_MODEL // 2
N_BRANCH = 4
A_W = BR_W
A_CONV = 3
HEAD_DIM = 128
N_HEADS = BR_W // HEAD_DIM
GROUPS = ((128, 1), (512, 4), (2048, 16))
N_GROUPS = len(GROUPS)
WINDOW_MAX = 2048
ROT_DIM = HEAD_DIM // 4
ROPE_THETA = 500000.0
Q_BLOCK = 128
C_W = BR_W
POOL_WINDOWS = (2, 4, 8, 16)
C_GROUP = C_W // len(POOL_WINDOWS)
POOL_PAST = POOL_WINDOWS[-1] - 1
D_W = BR_W
D_CONV = 31
EPS = 1e-6
IN_SIZES = (A_W, A_W, A_W, A_W,
            N_GROUPS * N_HEADS * HEAD_DIM, N_HEADS * HEAD_DIM, N_HEADS * HEAD_DIM, BR_W,
            C_W, C_W,
            2 * D_W, D_W,
            N_BRANCH * D_MODEL)
N_IN = sum(IN_SIZES)

kernel_name = 'gated_parallel_conv_dilatedattn_pool_conformer_decoder'


def in_split_points():
    pts, acc = [], 0
    for s in IN_SIZES[:-1]:
        acc += s
        pts.append(acc)
    return pts


def rms_norm(x, g):
    xf = x.astype(jnp.float32)
    y = xf * lax.rsqrt(jnp.mean(xf * xf, axis=-1, keepdims=True) + EPS)
    return (y * g.astype(jnp.float32)).astype(x.dtype)


def layer_norm(x, g, b):
    xf = x.astype(jnp.float32)
    mu = jnp.mean(xf, axis=-1, keepdims=True)
    xc = xf - mu
    var = jnp.mean(xc * xc, axis=-1, keepdims=True)
    y = xc * lax.rsqrt(var + EPS) * g.astype(jnp.float32) + b.astype(jnp.float32)
    return y.astype(x.dtype)


def rope_partial(x, pos):
    half = ROT_DIM // 2
    inv = ROPE_THETA ** (-(jnp.arange(half, dtype=jnp.float32) / half))
    ang = pos.astype(jnp.float32)[:, None] * inv[None, :]
    shape = (1, pos.shape[0]) + (1,) * (x.ndim - 3) + (half,)
    cos = jnp.cos(ang).reshape(shape)
    sin = jnp.sin(ang).reshape(shape)
    xr = x[..., :ROT_DIM].astype(jnp.float32)
    x1, x2 = xr[..., :half], xr[..., half:]
    rot = jnp.concatenate([x1 * cos - x2 * sin, x2 * cos + x1 * sin], axis=-1).astype(x.dtype)
    return jnp.concatenate([rot, x[..., ROT_DIM:]], axis=-1)


def causal_dwconv(u_ext, w):
    c = u_ext.shape[-1]
    return lax.conv_general_dilated(u_ext, w[:, None, :].astype(u_ext.dtype), window_strides=(1,),
                                    padding='VALID', dimension_numbers=('NWC', 'WIO', 'NWC'),
                                    feature_group_count=c)


def pool_mix(u_ext, pos, w_pool, scale):
    t_len = pos.shape[0]
    bsz = u_ext.shape[0]
    L = POOL_WINDOWS[-1]
    uf = u_ext.astype(jnp.float32)
    cs = jnp.concatenate([jnp.zeros_like(uf[:, :1]), jnp.cumsum(uf, axis=1)], axis=1)
    u_new = uf[:, POOL_PAST:]
    outs = []
    for g, w in enumerate(POOL_WINDOWS):
        sl = slice(g * C_GROUP, (g + 1) * C_GROUP)
        s = cs[:, L:, sl] - cs[:, L - w:L - w + t_len, sl]
        cnt = jnp.minimum(pos + 1, w).astype(jnp.float32)[None, :, None]
        outs.append(s / cnt - u_new[:, :, sl])
    p = jnp.stack(outs, axis=2)
    y = jnp.einsum('btgc,gcd->btgd', p, w_pool.astype(jnp.float32)).reshape(bsz, t_len, C_W)
    return (y * scale.astype(jnp.float32)).astype(u_ext.dtype)


def merge_groups(outs, lses):
    o = jnp.stack(outs, axis=0)
    a = jax.nn.softmax(jnp.stack(lses, axis=0), axis=0)
    return jnp.sum(a[..., None] * o, axis=0)


def dilated_block(qb, k_pad, v_pad, t0, w, d, pad):
    bsz = qb.shape[0]
    nj = Q_BLOCK // d
    nk = w // d + 1
    n_l = nk - 1 + nj
    qr = qb.reshape(bsz, nj, d, N_HEADS, HEAD_DIM).astype(jnp.float32)
    r = jnp.arange(d)
    i = jnp.arange(n_l)
    j = jnp.arange(nj)
    pos_k = t0 + r[:, None] + (i[None, :] - (nk - 1)) * d
    idx = pos_k + pad
    kg = k_pad[:, idx].astype(jnp.float32)
    vg = v_pad[:, idx].astype(jnp.float32)
    s = jnp.einsum('bjrhc,brihc->bhrji', qr, kg) * (HEAD_DIM ** -0.5)
    rel = i[None, :] - j[:, None]
    valid = ((rel >= 0) & (rel <= nk - 1))[None, :, :] & (pos_k >= 0)[:, None, :]
    s = jnp.where(valid[None, None], s, -jnp.inf)
    m = jnp.max(s, axis=-1, keepdims=True)
    p = jnp.exp(s - m)
    den = jnp.sum(p, axis=-1, keepdims=True)
    o = jnp.einsum('bhrji,brihc->bjrhc', p / den, vg).reshape(bsz, Q_BLOCK, N_HEADS, HEAD_DIM)
    lse = (m + jnp.log(den))[..., 0].transpose(0, 3, 2, 1).reshape(bsz, Q_BLOCK, N_HEADS)
    return o, lse


def dilated_attn_prompt(q, k, v):
    bsz, s_len = q.shape[:2]
    pad = WINDOW_MAX
    k_pad = jnp.pad(k, ((0, 0), (pad, 0), (0, 0), (0, 0)))
    v_pad = jnp.pad(v, ((0, 0), (pad, 0), (0, 0), (0, 0)))
    n_blk = s_len // Q_BLOCK
    q_blocks = q.reshape(bsz, n_blk, Q_BLOCK, N_GROUPS, N_HEADS, HEAD_DIM).transpose(1, 0, 2, 3, 4, 5)

    def one_block(args):
        qb, blk = args
        t0 = blk * Q_BLOCK
        outs, lses = [], []
        for g, (w, d) in enumerate(GROUPS):
            o, lse = dilated_block(qb[:, :, g], k_pad, v_pad, t0, w, d, pad)
            outs.append(o)
            lses.append(lse)
        return merge_groups(outs, lses).astype(q.dtype)

    o = lax.map(one_block, (q_blocks, jnp.arange(n_blk, dtype=jnp.int32)))
    return o.transpose(1, 0, 2, 3, 4).reshape(bsz, s_len, N_HEADS, HEAD_DIM)


def dilated_attn_sample(q, k_ext, v_ext):
    t_len = q.shape[1]
    n_rows = k_ext.shape[1] - t_len
    s_idx = jnp.arange(t_len)
    outs, lses = [], []
    for g, (w, d) in enumerate(GROUPS):
        nk = w // d + 1
        idx = n_rows + s_idx[:, None] - jnp.arange(nk)[None, :] * d
        valid = idx >= 0
        safe = jnp.maximum(idx, 0)
        kg = k_ext[:, safe].astype(jnp.float32)
        vg = v_ext[:, safe].astype(jnp.float32)
        s = jnp.einsum('bthc,btkhc->bhtk', q[:, :, g].astype(jnp.float32), kg) * (HEAD_DIM ** -0.5)
        s = jnp.where(valid[None, None], s, -jnp.inf)
        m = jnp.max(s, axis=-1, keepdims=True)
        p = jnp.exp(s - m)
        den = jnp.sum(p, axis=-1, keepdims=True)
        outs.append(jnp.einsum('bhtk,btkhc->bthc', p / den, vg))
        lses.append((m + jnp.log(den))[..., 0].transpose(0, 2, 1))
    return merge_groups(outs, lses).astype(q.dtype)


def mixer_layer(x, pos, past, wl):
    bsz, t_len, _ = x.shape
    xn = rms_norm(x, wl['norm_g'])
    h = jnp.einsum('btd,de->bte', xn, wl['w_in'])
    (va, ca, ba, za, q, k, v, zb, uc, zc, glu, zd, gt) = jnp.split(h, in_split_points(), axis=-1)

    ua = ca * va
    pa = jnp.zeros((bsz, A_CONV - 1, A_W), x.dtype) if past is None else past[2].astype(x.dtype)
    ua_ext = jnp.concatenate([pa, ua], axis=1)
    ya = ba * causal_dwconv(ua_ext, wl['a_conv_w']) * jax.nn.silu(za)
    new_a = ua_ext[:, -(A_CONV - 1):]

    q = rms_norm(q.reshape(bsz, t_len, N_GROUPS, N_HEADS, HEAD_DIM), wl['q_norm_g'])
    k = rms_norm(k.reshape(bsz, t_len, N_HEADS, HEAD_DIM), wl['k_norm_g'])
    v = v.reshape(bsz, t_len, N_HEADS, HEAD_DIM)
    q = rope_partial(q, pos)
    k = rope_partial(k, pos)
    if past is None:
        o = dilated_attn_prompt(q, k, v)
        rows = min(WINDOW_MAX, t_len)
        new_k, new_v = k[:, t_len - rows:], v[:, t_len - rows:]
    else:
        rows = past[0].shape[1]
        k_ext = jnp.concatenate([past[0].astype(x.dtype), k], axis=1)
        v_ext = jnp.concatenate([past[1].astype(x.dtype), v], axis=1)
        o = dilated_attn_sample(q, k_ext, v_ext)
        new_k, new_v = k_ext[:, -rows:], v_ext[:, -rows:]
    yb = o.reshape(bsz, t_len, BR_W) * jax.nn.silu(zb)

    pc = jnp.zeros((bsz, POOL_PAST, C_W), x.dtype) if past is None else past[3].astype(x.dtype)
    uc_ext = jnp.concatenate([pc, uc], axis=1)
    yc = pool_mix(uc_ext, pos, wl['c_pool_w'], wl['c_scale']) * jax.nn.silu(zc)
    new_c = uc_ext[:, -POOL_PAST:]

    ga, gb = jnp.split(glu, 2, axis=-1)
    ud = ga * jax.nn.sigmoid(gb)
    pd = jnp.zeros((bsz, D_CONV - 1, D_W), x.dtype) if past is None else past[4].astype(x.dtype)
    ud_ext = jnp.concatenate([pd, ud], axis=1)
    yd = causal_dwconv(ud_ext, wl['d_conv_w']) + wl['d_conv_b']
    yd = jax.nn.silu(layer_norm(yd, wl['d_ln_g'], wl['d_ln_b'])) * jax.nn.silu(zd)
    new_d = ud_ext[:, -(D_CONV - 1):]

    g = jax.nn.sigmoid(gt.reshape(bsz, t_len, N_BRANCH, D_MODEL))
    merged = (g[:, :, 0] * jnp.einsum('btc,cd->btd', ya, wl['w_br_a'])
              + g[:, :, 1] * jnp.einsum('btc,cd->btd', yb, wl['w_br_b'])
              + g[:, :, 2] * jnp.einsum('btc,cd->btd', yc, wl['w_br_c'])
              + g[:, :, 3] * jnp.einsum('btc,cd->btd', yd, wl['w_br_d']))
    out = x + jnp.einsum('btd,de->bte', merged, wl['w_out'])
    return out, (new_k, new_v, new_a, new_c, new_d)


def setup_inputs(seed: int = 0) -> dict:
    key = jax.random.key(seed)
    ks = jax.random.split(key, 24)

    def nrm(k, shape, scale):
        return jax.random.normal(k, shape, jnp.float32) * scale

    wbuf = min(WINDOW_MAX, PAST_LEN)
    return {
        'x_prompt': nrm(ks[0], (BATCH, SEQ, D_MODEL), 1.0),
        'x_sample': nrm(ks[1], (DEC_BATCH, DEC_SEQ, D_MODEL), 1.0),
        'cache_attn_k': nrm(ks[2], (DEPTH, DEC_BATCH, wbuf, N_HEADS, HEAD_DIM), 1.0),
        'cache_attn_v': nrm(ks[3], (DEPTH, DEC_BATCH, wbuf, N_HEADS, HEAD_DIM), 1.0),
        'state_conv_a': nrm(ks[4], (DEPTH, DEC_BATCH, A_CONV - 1, A_W), 1.0),
        'state_pool_c': nrm(ks[5], (DEPTH, DEC_BATCH, POOL_PAST, C_W), 1.0),
        'state_conv_d': nrm(ks[6], (DEPTH, DEC_BATCH, D_CONV - 1, D_W), 0.5),
        'norm_g': 1.0 + nrm(ks[7], (DEPTH, D_MODEL), 0.05),
        'w_in': nrm(ks[8], (DEPTH, D_MODEL, N_IN), D_MODEL ** -0.5),
        'q_norm_g': 1.0 + nrm(ks[9], (DEPTH, HEAD_DIM), 0.05),
        'k_norm_g': 1.0 + nrm(ks[10], (DEPTH, HEAD_DIM), 0.05),
        'a_conv_w': nrm(ks[11], (DEPTH, A_CONV, A_W), A_CONV ** -0.5),
        'c_pool_w': nrm(ks[12], (DEPTH, len(POOL_WINDOWS), C_GROUP, C_GROUP), C_GROUP ** -0.5),
        'c_scale': 1.0 + nrm(ks[13], (DEPTH, C_W), 0.05),
        'd_conv_w': nrm(ks[14], (DEPTH, D_CONV, D_W), D_CONV ** -0.5),
        'd_conv_b': nrm(ks[15], (DEPTH, D_W), 0.01),
        'd_ln_g': 1.0 + nrm(ks[16], (DEPTH, D_W), 0.05),
        'd_ln_b': nrm(ks[17], (DEPTH, D_W), 0.01),
        'w_br_a': nrm(ks[18], (DEPTH, A_W, D_MODEL), A_W ** -0.5),
        'w_br_b': nrm(ks[19], (DEPTH, BR_W, D_MODEL), BR_W ** -0.5),
        'w_br_c': nrm(ks[20], (DEPTH, C_W, D_MODEL), C_W ** -0.5),
        'w_br_d': nrm(ks[21], (DEPTH, D_W, D_MODEL), D_W ** -0.5),
        'w_out': nrm(ks[22], (DEPTH, D_MODEL, D_MODEL), D_MODEL ** -0.5),
    }


def reference(x_prompt, x_sample, cache_attn_k, cache_attn_v, state_conv_a, state_pool_c, state_conv_d,
              norm_g, w_in, q_norm_g, k_norm_g, a_conv_w, c_pool_w, c_scale, d_conv_w, d_conv_b,
              d_ln_g, d_ln_b, w_br_a, w_br_b, w_br_c, w_br_d, w_out):
    pos_p = jnp.arange(x_prompt.shape[1], dtype=jnp.int32)
    pos_s = PAST_LEN + jnp.arange(x_sample.shape[1], dtype=jnp.int32)
    hp, hs = x_prompt, x_sample
    st_p = [[], [], [], [], []]
    st_s = [[], [], [], [], []]
    for l in range(DEPTH):
        wl = {'norm_g': norm_g[l], 'w_in': w_in[l], 'q_norm_g': q_norm_g[l], 'k_norm_g': k_norm_g[l],
              'a_conv_w': a_conv_w[l], 'c_pool_w': c_pool_w[l], 'c_scale': c_scale[l],
              'd_conv_w': d_conv_w[l], 'd_conv_b': d_conv_b[l], 'd_ln_g': d_ln_g[l], 'd_ln_b': d_ln_b[l],
              'w_br_a': w_br_a[l], 'w_br_b': w_br_b[l], 'w_br_c': w_br_c[l], 'w_br_d': w_br_d[l],
              'w_out': w_out[l]}
        hp, sp = mixer_layer(hp, pos_p, None, wl)
        hs, ss = mixer_layer(hs, pos_s, (cache_attn_k[l], cache_attn_v[l], state_conv_a[l],
                                         state_pool_c[l], state_conv_d[l]), wl)
        for i in range(5):
            st_p[i].append(sp[i])
            st_s[i].append(ss[i])
    new_k_prompt = jnp.stack(st_p[0], axis=0)
    new_v_prompt = jnp.stack(st_p[1], axis=0)
    new_conv_a_prompt = jnp.stack(st_p[2], axis=0)
    new_pool_c_prompt = jnp.stack(st_p[3], axis=0)
    new_conv_d_prompt = jnp.stack(st_p[4], axis=0)
    new_k_sample = jnp.stack(st_s[0], axis=0)
    new_v_sample = jnp.stack(st_s[1], axis=0)
    new_conv_a_sample = jnp.stack(st_s[2], axis=0)
    new_pool_c_sample = jnp.stack(st_s[3], axis=0)
    new_conv_d_sample = jnp.stack(st_s[4], axis=0)
    return (hp, hs, new_k_prompt, new_v_prompt, new_conv_a_prompt, new_pool_c_prompt, new_conv_d_prompt,
            new_k_sample, new_v_sample, new_conv_a_sample, new_pool_c_sample, new_conv_d_sample)
```

```python
import os
import numpy as np
import ml_dtypes
import concourse.bass as bass
import concourse.mybir as mybir
from concourse.bass_utils import run_bass_kernel_spmd

F32 = mybir.dt.float32
BF16 = mybir.dt.bfloat16
AF = mybir.ActivationFunctionType
ALU = mybir.AluOpType
AX = mybir.AxisListType

DM = 2048
NIN = 23552
NL = 2
T = 1024
NT = 8
S = 4
TT = 1032
GW = 344
NKC = 16
(O_VA, O_CA, O_BA, O_ZA, O_Q, O_K, O_V, O_ZB, O_UC, O_ZC, O_GA, O_GB, O_ZD, O_GT) = (
    0, 1024, 2048, 3072, 4096, 7168, 8192, 9216, 10240, 11264, 12288, 13312, 14336, 15360)
HA, HC, HD = 2, 15, 30
EX_K, EX_V, EX_HA, EX_HC, EX_HD, R_EX = 0, 1024, 2048, 2050, 2065, 2096
EPS = 1e-6
PAST = 16384
P_NG, P_AW, P_CS, P_DW, P_DB, P_LG, P_LB, NPAR = 0, 16, 40, 48, 296, 304, 312, 320
NSLOT = 3
NDSEM = 40

STAGE = int(os.environ.get("KSTAGE", "99"))
KDBG = int(os.environ.get("KDBG", "0"))
KCORES = int(os.environ.get("KCORES", "8"))
KDUMP = int(os.environ.get("KDUMP", "0"))
KW0 = int(os.environ.get("KW0", "0"))
KW1 = int(os.environ.get("KW1", str(23552)))


class Res:
    __slots__ = ("name", "w", "r")

    def __init__(self, name):
        self.name = name
        self.w = None
        self.r = {}


class Sem:
    __slots__ = ("h", "idx", "val")

    def __init__(self, h, idx):
        self.h = h
        self.idx = idx
        self.val = 0


class Sched:
    def __init__(self, nc):
        self.nc = nc
        self.eng = {"pe": nc.tensor, "act": nc.scalar, "dve": nc.vector, "pool": nc.gpsimd, "sp": nc.sync}
        self.sem = {}
        n = 0
        for k in self.eng:
            self.sem[k] = Sem(nc.alloc_semaphore("sem_" + k), n)
            n += 1
        self.dsem = []
        for i in range(NDSEM):
            self.dsem.append(Sem(nc.alloc_semaphore("dsem%d" % i), n))
            n += 1
        self.ccsem = Sem(nc.alloc_semaphore("ccsem"), n)
        n += 1
        self.dnext = 0
        self.waited = {k: {} for k in self.eng}
        self.nwait = 0
        self.nop = 0

    def _wait(self, e, ev):
        sem, val = ev
        if e == "pe" and sem is self.sem["pe"]:
            return
        w = self.waited[e]
        if w.get(sem.idx, 0) < val:
            self.eng[e].wait_ge(sem.h, val)
            w[sem.idx] = val
            self.nwait += 1

    def deps(self, e, reads, writes):
        for r in reads:
            if r.w is not None:
                self._wait(e, r.w)
        for x in writes:
            if x.w is not None:
                self._wait(e, x.w)
            for idx, ev in x.r.items():
                self._wait(e, ev)

    def _record(self, ev, reads, writes):
        sem, val = ev
        for r in reads:
            old = r.r.get(sem.idx)
            if old is None or old[1] < val:
                r.r[sem.idx] = ev
        for x in writes:
            x.w = ev
            x.r = {}

    def op(self, e, fn, reads=(), writes=(), signal=True):
        self.deps(e, reads, writes)
        inst = fn(self.eng[e])
        sem = self.sem[e]
        if signal:
            sem.val += 1
            inst.then_inc(sem.h, 1)
            ev = (sem, sem.val)
        else:
            ev = (sem, sem.val + 1)
        self._record(ev, reads, writes)
        self.nop += 1
        return inst

    def dma(self, q, pairs, reads=(), writes=(), **kw):
        sem = self.dsem[self.dnext]
        self.dnext = (self.dnext + 1) % NDSEM
        if sem.val > 0:
            self._wait(q, (sem, sem.val))
        self.deps(q, reads, writes)
        for (o, i) in pairs:
            self.eng[q].dma_start(out=o, in_=i, **kw).then_inc(sem.h, 16)
            sem.val += 16
        ev = (sem, sem.val)
        self._record(ev, reads, writes)
        return ev

    def alias(self, new, old):
        for n_ in new:
            for o in old:
                if o.w is not None:
                    s, v = o.w
                    cur = n_.r.get(s.idx)
                    if cur is None or cur[1] < v:
                        n_.r[s.idx] = o.w
                for idx, ev in o.r.items():
                    cur = n_.r.get(idx)
                    if cur is None or cur[1] < ev[1]:
                        n_.r[idx] = ev


def mkap(base, dims):
    return bass.AP(base.tensor, base.offset, [list(base.ap[0])] + [list(d) for d in dims])


class Prog:
    def __init__(self):
        nc = bass.Bass("TRN2", target_bir_lowering=False)
        self.nc = nc
        self.sc = Sched(nc)
        self.declare_dram()
        self.alloc()
        self.setup()
        for l in range(NL if STAGE >= 99 else 1):
            self.layer(l)
        self.finish()

    def declare_dram(self):
        nc = self.nc

        def din(name, shape, dt=F32):
            return nc.dram_tensor(name, list(shape), dt, kind="ExternalInput").ap()

        def dout(name, shape, dt=F32):
            return nc.dram_tensor(name, list(shape), dt, kind="ExternalOutput").ap()

        self.xp = din("xp", [T, DM])
        self.xs = din("xs", [S, DM])
        self.ck = din("ck", [NL, 2048, 1024])
        self.cv = din("cv", [NL, 2048, 1024])
        self.sa = din("sa", [NL, HA, 1024])
        self.scs = din("scs", [NL, HC, 1024])
        self.sd = din("sd", [NL, HD, 1024])
        self.w_in_full = din("w_in", [NL, DM, KW1 - KW0])
        self.w_br = din("w_br", [NL, 4, 1024, DM])
        self.w_out = din("w_out", [NL, DM, DM])
        self.w_pool = din("w_pool", [NL, 4, 256, 256])
        self.params = din("params", [NL, 128, NPAR])
        self.gq = din("gq", [NL, 128, 128])
        self.gk = din("gk", [NL, 128, 128])
        self.ropec = din("ropec", [128, 9, 32])
        self.ropes = din("ropes", [128, 9, 32])
        self.m1 = din("m1", [128, 256], BF16)
        self.m1f = din("m1f", [128, 256], BF16)
        self.m16 = din("m16", [128, 64], BF16)
        self.msk_s = din("msk_s", [128, 17, 12], BF16)
        self.flag = din("flag", [128, 1])
        self.invc = din("invc", [128, 4, 16])

        self.yp = dout("yp", [T, DM])
        self.ys = dout("ys", [S, DM])
        self.nkp = dout("nkp", [NL, T, 1024])
        self.nvp = dout("nvp", [NL, T, 1024])
        self.ncap = dout("ncap", [NL, HA, 1024])
        self.npcp = dout("npcp", [NL, HC, 1024])
        self.ncdp = dout("ncdp", [NL, HD, 1024])
        self.nks = dout("nks", [NL, 2048, 1024])
        self.nvs = dout("nvs", [NL, 2048, 1024])
        self.ncas = dout("ncas", [NL, HA, 1024])
        self.npcs = dout("npcs", [NL, HC, 1024])
        self.ncds = dout("ncds", [NL, HD, 1024])

        self.dbg = dout("dbg", [4, 128, 8, 1062], BF16) if KDUMP else None
        self.x1 = nc.dram_tensor("x1", [T + S, DM], F32, kind="Internal").ap()
        self.exs_k = nc.dram_tensor("exs_k", [1024, 1024], BF16, kind="Internal").ap()
        self.exs_v = nc.dram_tensor("exs_v", [1024, 1024], BF16, kind="Internal").ap()
        self.exs_h = nc.dram_tensor("exs_h", [48, 1024], BF16, kind="Internal").ap()
        self.exd_k = nc.dram_tensor("exd_k", [2048, 1024], BF16, kind="Internal").ap()
        self.exd_v = nc.dram_tensor("exd_v", [2048, 1024], BF16, kind="Internal").ap()
        self.exd_h = nc.dram_tensor("exd_h", [96, 1024], BF16, kind="Internal").ap()
        self.r_exs_k, self.r_exs_v, self.r_exs_h = Res("exs_k"), Res("exs_v"), Res("exs_h")
        self.r_exd_k, self.r_exd_v, self.r_exd_h = Res("exd_k"), Res("exd_v"), Res("exd_h")
        self.r_x1 = Res("x1")
        self.out_res = Res("outs")

    def wi(self, l, col0, ncol):
        assert KW0 <= col0 and col0 + ncol <= KW1, (col0, ncol)
        return self.w_in_full[l, :, col0 - KW0:col0 - KW0 + ncol]

    def alloc(self):
        nc = self.nc
        self.sb_off = (nc.sbuf_base + 63) // 64 * 64
        self.sb_lim = nc.sbuf_top

        def A(name, shape, dt, off=None):
            nbytes = int(np.prod(shape[1:])) * (4 if dt == F32 else 2)
            nbytes = (nbytes + 31) // 32 * 32
            if off is None:
                off = self.sb_off
                self.sb_off += nbytes
                assert self.sb_off <= self.sb_lim, ("SBUF overflow", name, self.sb_off, self.sb_lim)
            return nc.alloc_sbuf_tensor_at(name, list(shape), dt, offset=off)
        self.A = A

        self.XNT = A("XNT", [128, NKC, TT], BF16)
        self.UA = A("UA", [128, 8, HA + TT], BF16)
        self.UC = A("UC", [128, 8, HC + TT + 1], BF16)
        self.UD = A("UD", [128, 8, HD + TT], BF16)
        self.YB = A("YB", [128, 8, TT], BF16)
        self.r_xnt = Res("xnt")
        self.r_ua = [Res("ua%d" % c) for c in range(8)]
        self.r_uc = [Res("uc%d" % c) for c in range(8)]
        self.r_ud = [Res("ud%d" % c) for c in range(8)]
        self.r_yb = [Res("yb%d" % c) for c in range(8)]
        self.WS = [A("WS%d" % i, [128, NKC, 256], BF16) for i in range(NSLOT)]
        self.r_ws = [Res("ws%d" % i) for i in range(NSLOT)]
        self.nslab = 0
        self.identb = A("identb", [128, 128], BF16)
        self.identf = A("identf", [128, 128], F32)
        self.onesb = A("onesb", [128, 128], BF16)
        self.onesm = A("onesm", [128, 128], BF16)
        self.M1 = A("M1", [128, 256], BF16)
        self.M1F = A("M1F", [128, 256], BF16)
        self.M16 = A("M16", [128, 64], BF16)
        self.MS = A("MS", [128, 17, 12], BF16)
        self.FLAG = A("FLAG", [128, 1], F32)
        self.EPSC = A("EPSC", [128, 1], F32)
        self.INVC = A("INVC", [128, 4, 16], F32)
        self.RC = A("RC", [128, 9, 32], F32)
        self.RS = A("RS", [128, 9, 32], F32)
        self.PAR = [A("PAR%d" % l, [128, NPAR], F32) for l in range(NL)]
        self.GQ = [A("GQ%d" % l, [128, 128], F32) for l in range(NL)]
        self.GK = [A("GK%d" % l, [128, 128], F32) for l in range(NL)]
        self.r_const = Res("const")
        self.UAL = A("UAL", [128, 8, 6], F32)
        self.UCL = A("UCL", [128, 8, 19], F32)
        self.UDL = A("UDL", [128, 8, 34], F32)
        self.ESA = A("ESA", [128, 8, HA + 4], F32)
        self.ESC = A("ESC", [128, 8, HC + 4], F32)
        self.ESD = A("ESD", [128, 8, HD + 4], F32)
        self.CSA = A("CSA", [128, 8, 4], F32)
        self.CSD = A("CSD", [128, 8, 4], F32)
        self.CST = A("CST", [128, 8, 4], F32)
        self.PL = [A("PL%d" % i, [128, 8, HC + 4], F32) for i in range(2)]
        self.PSS = A("PSS", [128, 8, 4], F32)
        self.ROWB = A("ROWB", [34, 1024], F32)
        self.STIN = A("STIN", [30, 1024], F32)
        self.KSB = A("KSB", [4, 1024], BF16)
        self.VSB = A("VSB", [4, 1024], BF16)
        self.HALO = A("HALO", [128, 8, 47], BF16)
        self.r_ual, self.r_ucl, self.r_udl = Res("ual"), Res("ucl"), Res("udl")
        self.r_esa, self.r_esc, self.r_esd = Res("esa"), Res("esc"), Res("esd")
        self.r_csa, self.r_csd, self.r_cst, self.r_pss = Res("csa"), Res("csd"), Res("cst"), Res("pss")
        self.r_pl = [Res("pl0"), Res("pl1")]
        self.r_rowb, self.r_stin = Res("rowb"), Res("stin")
        self.r_ksb, self.r_vsb, self.r_halo = Res("ksb"), Res("vsb"), Res("halo")
        self.SMALL = A("SMALL", [128, 64], F32)
        self.RTMP = A("RTMP", [128, 4, 32], F32)
        self.RTA = A("RTA", [128, 4, 32], F32)
        self.r_small = Res("small")

        xoff = self.sb_off
        self.MERGED = A("MERGED", [128, NKC, TT], BF16)
        self.r_merged = [Res("mg%d" % c) for c in range(NKC)]
        xend = self.sb_off
        self.sb_off = xoff
        self.KC = A("KC", [128, 16, 128], BF16)
        self.VC = A("VC", [128, 16, 128], BF16)
        self.KT = A("KT", [128, 16, 128], BF16)
        self.QT = A("QT", [128, 3, TT], BF16)
        self.QS = A("QS", [128, 12], BF16)
        self.NUM = A("NUM", [128, TT], F32)
        self.DEN = A("DEN", [128, TT], F32)
        self.PT = [A("PT%d" % i, [128, 256], BF16) for i in range(2)]
        self.PTS = A("PTS", [128, 17, 12], BF16)
        self.QF = [A("QF%d" % i, [128, 3, 128], F32) for i in range(2)]
        self.QSQ = A("QSQ", [128, 3, 128], F32)
        self.QB = [A("QB%d" % i, [128, 3, 128], BF16) for i in range(2)]
        self.KTN = A("KTN", [128, 4], BF16)
        self.QTN = [A("QTN%d" % i, [128, 384], BF16) for i in range(2)]
        self.r_qtn = [Res("qtn0"), Res("qtn1")]
        self.OSS = A("OSS", [128, 2, 12], F32)
        self.sb_off = max(xend, self.sb_off)
        self.r_kc, self.r_vc, self.r_kt = Res("kc"), Res("vc"), Res("kt")
        self.r_qt = [Res("qt%d" % g) for g in range(3)]
        self.r_qs, self.r_num, self.r_den = Res("qs"), Res("num"), Res("den")
        self.r_pt = [Res("pt0"), Res("pt1")]
        self.r_pts = Res("pts")
        self.r_qf = [Res("qf0"), Res("qf1")]
        self.r_qsq = Res("qsq")
        self.r_qb = [Res("qb0"), Res("qb1")]
        self.r_ktn, self.r_oss = Res("ktn"), Res("oss")
        self.attn_res = [self.r_kc, self.r_vc, self.r_kt] + self.r_qt + [self.r_qs, self.r_num, self.r_den] + \
            self.r_pt + [self.r_pts] + self.r_qf + [self.r_qsq] + self.r_qb + [self.r_ktn, self.r_oss] + self.r_qtn

        toff = self.sb_off
        TSIZE = 21 * 1024
        self.sb_off += TSIZE
        assert self.sb_off <= self.sb_lim, ("SBUF overflow T", self.sb_off, self.sb_lim)

        def TA(base):
            st = {"o": base}

            def f(name, shape, dt):
                t = A(name, shape, dt, off=st["o"])
                nb = (int(np.prod(shape[1:])) * (4 if dt == F32 else 2) + 31) // 32 * 32
                st["o"] += nb
                assert st["o"] <= toff + TSIZE, ("T region overflow", name)
                return t
            return f
        f = TA(toff)
        self.XST = [f("XST%d" % i, [128, DM], F32) for i in range(2)]
        self.XJ = f("XJ", [128, DM], BF16)
        self.r_xst = [Res("xst0"), Res("xst1")]
        self.r_xj = Res("xj")
        f = TA(toff)
        self.KSQ = f("KSQ", [128, 512], F32)
        self.KN = [f("KN%d" % i, [128, 512], F32) for i in range(2)]
        self.KB = [f("KB%d" % i, [128, 512], BF16) for i in range(2)]
        self.r_ksq = Res("ksq")
        self.r_kn = [Res("kn0"), Res("kn1")]
        self.r_kb = [Res("kb0"), Res("kb1")]
        self.r_rtmp, self.r_rta = Res("rtmp"), Res("rta")
        f = TA(toff)
        self.TF = [f("TF%d" % i, [128, TT + 16], F32) for i in range(4)]
        self.TB = [f("TB%d" % i, [128, TT], BF16) for i in range(2)]
        self.r_tf = [Res("tf%d" % i) for i in range(4)]
        self.r_tb = [Res("tb0"), Res("tb1")]
        f = TA(toff)
        self.XSL = [f("XSL%d" % i, [128, 512], F32) for i in range(2)]
        self.OSL = [f("OSL%d" % i, [128, 512], F32) for i in range(2)]
        self.r_xsl = [Res("xsl0"), Res("xsl1")]
        self.r_osl = [Res("osl0"), Res("osl1")]
        self.tres = {
            "p0": self.r_xst + [self.r_xj],
            "p1a": [self.r_ksq] + self.r_kn + self.r_kb,
            "p2": self.r_tf + self.r_tb,
            "p4": self.r_xsl + self.r_osl,
        }
        self.tcur = None
        self.sbuf_used = self.sb_off

        self.psA = nc.alloc_psum_tensor("psA", [128, 3, 512], F32)
        self.psB = nc.alloc_psum_tensor("psB", [128, 3, 512], F32)
        self.ps6 = nc.alloc_psum_tensor("ps6", [128, 512], F32)
        self.ps7 = nc.alloc_psum_tensor("ps7", [128, 512], F32)
        self.r_bank = [Res("bank%d" % i) for i in range(8)]

    def tphase(self, name):
        if self.tcur == name:
            return
        if self.tcur is not None:
            self.sc.alias(self.tres[name], self.tres[self.tcur])
        self.tcur = name

    def bank(self, b):
        if b < 3:
            return self.psA[:, b, :]
        if b < 6:
            return self.psB[:, b - 3, :]
        return (self.ps6 if b == 6 else self.ps7)[:, :]

    def unit(self, u):
        return (self.psA if u == 0 else self.psB), [self.r_bank[3 * u + j] for j in range(3)]

    def setup(self):
        nc, sc = self.nc, self.sc
        rc = self.r_const
        iot = self.SMALL
        self.tphase("p2")
        io = self.TF[0]
        sc.op("pool", lambda e: e.iota(io[:, 0:128], [[1, 128]], base=0, channel_multiplier=-1,
                                       allow_small_or_imprecise_dtypes=True), [], [self.r_tf[0]])
        sc.op("dve", lambda e: e.tensor_single_scalar(out=self.identb[:], in_=io[:, 0:128], scalar=0.0,
                                                      op=ALU.is_equal), [self.r_tf[0]], [rc])
        sc.op("dve", lambda e: e.tensor_single_scalar(out=self.identf[:], in_=io[:, 0:128], scalar=0.0,
                                                      op=ALU.is_equal), [self.r_tf[0]], [rc])
        sc.op("dve", lambda e: e.memset(self.onesb[:], 1.0), [], [rc])
        sc.op("dve", lambda e: e.memset(self.onesm[:], 1.0 / 1024.0), [], [rc])
        sc.op("dve", lambda e: e.memset(self.EPSC[:], EPS), [], [rc])
        sc.op("dve", lambda e: e.memset(self.XNT[:, :, 1028:1032], 0.0), [], [self.r_xnt])
        sc.op("dve", lambda e: e.memset(self.NUM[:, 1028:1032], 1.0), [], [self.r_num])
        sc.op("dve", lambda e: e.memset(self.DEN[:, 1028:1032], 1.0), [], [self.r_den])
        pairs = [(self.M1[:], self.m1), (self.M1F[:], self.m1f), (self.M16[:], self.m16), (self.MS[:], self.msk_s),
                 (self.FLAG[:], self.flag), (self.INVC[:], self.invc), (self.RC[:], self.ropec), (self.RS[:], self.ropes)]
        for l in range(NL):
            pairs += [(self.PAR[l][:], self.params[l]), (self.GQ[l][:], self.gq[l]), (self.GK[l][:], self.gk[l])]
        sc.dma("sp", pairs, [], [rc])
        for l in range(NL):
            for (dst, src) in ((self.nks, self.ck), (self.nvs, self.cv)):
                sc.dma("sp", [(dst[l, 0:2044, :], src[l, 4:2048, :])], [], [self.out_res])
            sc.dma("sp", [(self.npcs[l, 0:11, :], self.scs[l, 4:15, :]), (self.ncds[l, 0:26, :], self.sd[l, 4:30, :])],
                   [], [self.out_res])

    def load_slab(self, pieces, nk):
        slot = self.nslab % NSLOT
        self.nslab += 1
        ws = self.WS[slot]
        pairs = []
        for (src, c0) in pieces:
            ncol = src.shape[1]
            pairs.append((ws[:, 0:nk, c0:c0 + ncol], src.rearrange("(k p) n -> p k n", p=128)))
        self.sc.dma("pool", pairs, [], [self.r_ws[slot]])
        return slot

    def run_items(self, items):
        first = []
        n = 0
        for (loads, fn) in items:
            first.append(n)
            n += len(loads)
        flat = [ld for (loads, fn) in items for ld in loads]
        slots = [None] * len(flat)
        issued = 0
        for i, (loads, fn) in enumerate(items):
            a = first[i]
            b = a + len(loads)
            lim = min(len(flat), max(b, a + NSLOT))
            while issued < lim:
                slots[issued] = flat[issued]()
                issued += 1
            fn(slots[a:b])

    def fm_matmuls(self, u, slot, cb, nk, rhs_fn):
        ps, rb = self.unit(u)
        ws = self.WS[slot]
        for kc in range(nk):
            for g in range(3):
                last = (kc == nk - 1 and g == 2)
                rhs, rres = rhs_fn(kc, g)
                self.sc.op("pe", lambda e, kc=kc, g=g, rhs=rhs: e.matmul(
                    ps[:, g, 0:GW], lhsT=ws[:, kc, cb * 128:(cb + 1) * 128], rhs=rhs,
                    start=(kc == 0), stop=(kc == nk - 1)),
                    [self.r_ws[slot]] + rres, [rb[g]], signal=last)

    def xnt_rhs(self, kc, g):
        return self.XNT[:, kc, g * GW:(g + 1) * GW], [self.r_xnt]

    def uview(self, u):
        ps, rb = self.unit(u)
        return ps[:, :, 0:GW], rb

    @staticmethod
    def v3(ap2d):
        return ap2d.rearrange("p (g n) -> p g n", g=3)

    def tiles(self):
        for i in range(9):
            if i < 8:
                yield i, i * 128, 128
            else:
                yield i, 1024, 4

    def layer(self, l):
        steps = [self.phase0, self.phase1a, self.phase1b,
                 lambda l: self.collective(self.exs_h, self.exd_h, self.r_exs_h, self.r_exd_h),
                 self.phase2B, self.halos,
                 self.phase2A, self.phase2C, self.phase2D, self.dump, self.phase3, self.phase4]
        stage_of = [0, 1, 2, 3, 7, 3, 4, 5, 6, 0, 8, 9]
        kskip = int(os.environ.get("KSKIP", "0"))
        for idx, (fn, st) in enumerate(zip(steps, stage_of)):
            if kskip & (1 << idx):
                continue
            if fn == self.dump or STAGE >= st:
                fn(l)

    def dump(self, l):
        if KDUMP and l == 0:
            self.sc.dma("sp", [(self.dbg[0, :, :, 0:HA + TT], self.UA[:, :, :]), (self.dbg[1, :, :, 0:TT], self.YB[:, :, :]),
                               (self.dbg[2, :, :, 0:HC + TT], self.UC[:, :, 0:HC + TT]), (self.dbg[3, :, :, 0:HD + TT], self.UD[:, :, :])],
                        self.r_ua + self.r_yb + self.r_uc + self.r_ud, [self.out_res])

    def phase0(self, l):
        sc = self.sc
        self.tphase("p0")
        par = self.PAR[l]
        for i, tok0, rows in self.tiles():
            xb = self.XST[i % 2]
            rx = self.r_xst[i % 2]
            if l == 0:
                src = self.xp[tok0:tok0 + rows, :] if i < 8 else self.xs[:, :]
                rsrc = []
            else:
                src = self.x1[tok0:tok0 + rows, :]
                rsrc = [self.r_x1]
            sc.dma("sp", [(xb[0:rows, :], src)], rsrc, [rx])
            sm = self.SMALL
            sc.op("act", lambda e: e.activation(out=self.XJ[0:rows, :], in_=xb[0:rows, :], func=AF.Square,
                                                accum_out=sm[0:rows, 0:1]), [rx], [self.r_xj, self.r_small])
            sc.op("dve", lambda e: e.tensor_scalar(out=sm[0:rows, 1:2], in0=sm[0:rows, 0:1], scalar1=1.0 / DM,
                                                   scalar2=EPS, op0=ALU.mult, op1=ALU.add),
                  [self.r_small], [self.r_small])
            sc.op("act", lambda e: e.activation(out=sm[0:rows, 2:3], in_=sm[0:rows, 1:2], func=AF.Sqrt),
                  [self.r_small], [self.r_small])
            sc.op("dve", lambda e: e.reciprocal(out=sm[0:rows, 3:4], in_=sm[0:rows, 2:3]),
                  [self.r_small], [self.r_small])
            sc.op("dve", lambda e: e.tensor_scalar(out=xb[0:rows, :], in0=xb[0:rows, :], scalar1=sm[0:rows, 3:4],
                                                   scalar2=None, op0=ALU.mult), [self.r_small, rx], [rx])
            for q in range(4):
                b = 6 + (q % 2)
                pb = self.bank(b)
                for j in range(4):
                    kc = q * 4 + j
                    sc.op("pe", lambda e, kc=kc, j=j, pb=pb: e.transpose(
                        out=pb[:, j * 128:j * 128 + rows], in_=xb[0:rows, kc * 128:(kc + 1) * 128],
                        identity=self.identf[0:rows, 0:rows]), [rx, self.r_const], [self.r_bank[b]])
                for j in range(4):
                    kc = q * 4 + j
                    eng = "act" if j % 2 == 0 else "dve"
                    if eng == "act":
                        sc.op("act", lambda e, kc=kc, j=j, pb=pb: e.activation(
                            out=self.XNT[:, kc, tok0:tok0 + rows], in_=pb[:, j * 128:j * 128 + rows],
                            func=AF.Copy, scale=par[:, P_NG + kc:P_NG + kc + 1]),
                            [self.r_bank[b], self.r_const], [self.r_xnt])
                    else:
                        sc.op("dve", lambda e, kc=kc, j=j, pb=pb: e.tensor_scalar(
                            out=self.XNT[:, kc, tok0:tok0 + rows], in0=pb[:, j * 128:j * 128 + rows],
                            scalar1=par[:, P_NG + kc:P_NG + kc + 1], scalar2=None, op0=ALU.mult),
                            [self.r_bank[b], self.r_const], [self.r_xnt])

    def norm_rope(self, src_ps, rows, nh, gain, i, dst, r_dst, rbank):
        sc = self.sc
        sm = self.SMALL
        sq = self.KSQ if nh != 3 else self.QSQ
        rsq = self.r_ksq if nh != 3 else self.r_qsq
        src3 = src_ps.rearrange("p (h c) -> p h c", h=nh)
        sq3 = (sq[0:rows, 0:nh * 128].rearrange("p (h c) -> p h c", h=nh)) if nh != 3 else sq[0:rows, :, :]
        sc.op("act", lambda e: e.activation(out=sq3, in_=src3, func=AF.Square), rbank, [rsq])
        sc.op("dve", lambda e: e.tensor_reduce(out=sm[0:rows, 8:8 + nh], in_=sq3, axis=AX.X, op=ALU.add),
              [rsq], [self.r_small])
        sc.op("act", lambda e: e.activation(out=sm[0:rows, 16:16 + nh], in_=sm[0:rows, 8:8 + nh], func=AF.Sqrt,
                                            bias=self.EPSC[0:rows, 0:1], scale=1.0 / 128), [self.r_small, self.r_const], [self.r_small])
        sc.op("dve", lambda e: e.reciprocal(out=sm[0:rows, 24:24 + nh], in_=sm[0:rows, 16:16 + nh]),
              [self.r_small], [self.r_small])
        for h in range(nh):
            sc.op("dve", lambda e, h=h: e.scalar_tensor_tensor(out=dst[:, h, :], in0=src3[:, h, :], scalar=sm[0:rows, 24 + h:25 + h],
                                                             in1=gain[0:rows, :], op0=ALU.mult, op1=ALU.mult),
                  rbank + [self.r_small, self.r_const], [r_dst])
        ta = self.RTA[0:rows, 0:nh, :]
        tb = self.RTMP[0:rows, 0:nh, :]
        cc = mkap(self.RC[0:rows, i, 0:1], [[0, nh], [1, 32]])
        s_lo = mkap(self.RS[0:rows, i, 0:1], [[0, nh], [1, 16]])
        s_hi = mkap(self.RS[0:rows, i, 16:17], [[0, nh], [1, 16]])
        sc.op("dve", lambda e: e.tensor_tensor(out=ta, in0=dst[:, :, 0:32], in1=cc, op=ALU.mult),
              [r_dst, self.r_const], [self.r_rta])
        sc.op("dve", lambda e: e.tensor_tensor(out=tb[:, :, 0:16], in0=dst[:, :, 16:32], in1=s_lo, op=ALU.mult),
              [r_dst, self.r_const], [self.r_rtmp])
        sc.op("dve", lambda e: e.tensor_tensor(out=tb[:, :, 16:32], in0=dst[:, :, 0:16], in1=s_hi, op=ALU.mult),
              [r_dst, self.r_const], [self.r_rtmp])
        sc.op("dve", lambda e: e.tensor_tensor(out=dst[:, :, 0:32], in0=ta, in1=tb, op=ALU.add),
              [self.r_rta, self.r_rtmp], [r_dst])

    def phase1a(self, l):
        sc = self.sc
        self.tphase("p1a")
        items = []
        cnt = {"n": 0}
        for s in range(4):
            colb = (O_K if s < 2 else O_V) + (s % 2) * 512
            lda = (lambda colb=colb: self.load_slab([(self.wi(l, colb, 256), 0)], NKC))
            ldb = (lambda colb=colb: self.load_slab([(self.wi(l, colb + 256, 256), 0)], NKC))

            def fn(slots, s=s):
                isk = s < 2
                c0 = (s % 2) * 512
                for i, tok0, rows in self.tiles():
                    n = cnt["n"]
                    cnt["n"] += 1
                    b = n % 4
                    pb = self.bank(b)
                    rb = [self.r_bank[b]]
                    for hf, slot in enumerate(slots):
                        ws = self.WS[slot]
                        for kc in range(NKC):
                            sc.op("pe", lambda e, kc=kc, ws=ws, hf=hf: e.matmul(
                                pb[0:rows, hf * 256:(hf + 1) * 256], lhsT=self.XNT[:, kc, tok0:tok0 + rows], rhs=ws[:, kc, 0:256],
                                start=(kc == 0), stop=(kc == NKC - 1)),
                                [self.r_xnt, self.r_ws[slot]], rb, signal=(kc == NKC - 1 and hf == 1))
                    kn = self.KN[n % 2]
                    rkn = self.r_kn[n % 2]
                    kb = self.KB[n % 2]
                    rkb = self.r_kb[n % 2]
                    if isk:
                        dst = kn[0:rows, :].rearrange("p (h c) -> p h c", h=4)
                        self.norm_rope(pb[0:rows, 0:512], rows, 4, self.GK[l], i, dst, rkn, rb)
                        outd, ext_, rext = (self.nkp, self.exs_k, self.r_exs_k)
                    else:
                        sc.op("act", lambda e: e.activation(out=kn[0:rows, :], in_=pb[0:rows, 0:512], func=AF.Copy),
                              rb, [rkn])
                        outd, ext_, rext = (self.nvp, self.exs_v, self.r_exs_v)
                    sc.op("act", lambda e: e.activation(out=kb[0:rows, :], in_=kn[0:rows, :], func=AF.Copy),
                          [rkn], [rkb])
                    if i < 8:
                        sc.dma("sp", [(outd[l, tok0:tok0 + rows, c0:c0 + 512], kn[0:rows, :])], [rkn], [self.out_res])
                        sc.dma("sp", [(ext_[tok0:tok0 + rows, c0:c0 + 512], kb[0:rows, :])], [rkb], [rext])
                    else:
                        outs = self.nks if isk else self.nvs
                        sc.dma("sp", [(outs[l, 2044:2048, c0:c0 + 512], kn[0:rows, :])], [rkn], [self.out_res])
                        sbt, rsb = (self.KSB, self.r_ksb) if isk else (self.VSB, self.r_vsb)
                        sc.op("act", lambda e: e.activation(out=sbt[0:4, c0:c0 + 512], in_=kn[0:rows, :],
                                                            func=AF.Copy), [rkn], [rsb])
            items.append(([lda, ldb], fn))
        self.run_items(items)
        self.collective(self.exs_k, self.exd_k, self.r_exs_k, self.r_exd_k)
        self.collective(self.exs_v, self.exd_v, self.r_exs_v, self.r_exd_v)

    def ex_view(self, ap, row0, c, t):
        return bass.AP(ap.tensor, row0 * 1024, [[c * t, 128], [t, c], [1, t]])

    def state_out(self, src, n, dsts):
        sc = self.sc
        for hb in range(2):
            b = 6 + hb
            pb = self.bank(b)
            for j in range(4):
                c = hb * 4 + j
                sc.op("pe", lambda e, c=c, j=j: e.transpose(out=pb[0:n, j * 128:(j + 1) * 128], in_=src[:, c, 0:n],
                                                          identity=self.identf[:, :]),
                      [self.r_const, self.r_ual, self.r_ucl, self.r_udl], [self.r_bank[b]])
            sc.op("act", lambda e: e.activation(out=self.ROWB[0:n, hb * 512:(hb + 1) * 512], in_=pb[0:n, :],
                                                func=AF.Copy), [self.r_bank[b]], [self.r_rowb])
        pairs = [(d, self.ROWB[r0:r1, :]) for (d, r0, r1) in dsts]
        sc.dma("sp", pairs, [self.r_rowb], [self.out_res])

    def state_in(self, src_dram, n, dst):
        sc = self.sc
        sc.dma("sp", [(self.STIN[0:n, :], src_dram)], [], [self.r_stin])
        pb = self.bank(7)
        for c in range(8):
            sc.op("pe", lambda e, c=c: e.transpose(out=pb[:, c * 32:c * 32 + n], in_=self.STIN[0:n, c * 128:(c + 1) * 128],
                                                    identity=self.identf[0:n, 0:n]),
                  [self.r_const, self.r_stin], [self.r_bank[7]])
        src = pb[:, 0:256].rearrange("p (c t) -> p c t", c=8)[:, :, 0:n]
        sc.op("act", lambda e: e.activation(out=dst[:, :, 0:n], in_=src, func=AF.Copy),
              [self.r_bank[7]], [self.r_esa, self.r_esc, self.r_esd])

    def phase1b(self, l):
        sc = self.sc
        self.tphase("p2")
        par = self.PAR[l]
        self.state_in(self.sa[l], HA, self.ESA)
        self.state_in(self.scs[l], HC, self.ESC)
        self.state_in(self.sd[l], HD, self.ESD)
        items = []
        for c in range(8):
            ld = (lambda c=c: self.load_slab([(self.wi(l, O_VA + c * 128, 128), 0), (self.wi(l, O_CA + c * 128, 128), 128)], NKC))

            def fnA(slots, c=c):
                slot = slots[0]
                self.fm_matmuls(0, slot, 0, NKC, self.xnt_rhs)
                self.fm_matmuls(1, slot, 1, NKC, self.xnt_rhs)
                u0, rb0 = self.uview(0)
                u1, rb1 = self.uview(1)
                tf, rtf = self.TF[c % 2], self.r_tf[c % 2]
                sc.op("act", lambda e: e.activation(out=self.v3(tf[:, 0:TT]), in_=u0, func=AF.Copy), rb0, [rtf])
                sc.op("dve", lambda e: e.tensor_tensor(out=self.v3(self.UA[:, c, HA:HA + TT]), in0=u1,
                                                       in1=self.v3(tf[:, 0:TT]), op=ALU.mult), rb1 + [rtf], [self.r_ua[c]])
                sc.op("dve", lambda e: e.tensor_tensor(out=self.UAL[:, c, 0:6], in0=self.psB[:, 2, 334:340],
                                                       in1=tf[:, 1022:1028], op=ALU.mult), rb1 + [rtf], [self.r_ual])
            items.append(([ld], fnA))
        for cp in range(4):
            ld = (lambda cp=cp: self.load_slab([(self.wi(l, O_UC + cp * 256, 256), 0)], NKC))

            def fnC(slots, cp=cp):
                slot = slots[0]
                for j in range(2):
                    c = cp * 2 + j
                    self.fm_matmuls(j, slot, j, NKC, self.xnt_rhs)
                    u, rb = self.uview(j)
                    ps, _ = self.unit(j)
                    sc.op("act", lambda e: e.activation(out=self.v3(self.UC[:, c, HC:HC + TT]), in_=u, func=AF.Copy),
                          rb, [self.r_uc[c]])
                    sc.op("dve", lambda e: e.tensor_copy(out=self.UCL[:, c, 0:19], in_=ps[:, 2, 321:340]),
                          rb, [self.r_ucl])
            items.append(([ld], fnC))
        for c in range(8):
            ld = (lambda c=c: self.load_slab([(self.wi(l, O_GA + c * 128, 128), 0), (self.wi(l, O_GB + c * 128, 128), 128)], NKC))

            def fnD(slots, c=c):
                slot = slots[0]
                self.fm_matmuls(0, slot, 0, NKC, self.xnt_rhs)
                self.fm_matmuls(1, slot, 1, NKC, self.xnt_rhs)
                u0, rb0 = self.uview(0)
                u1, rb1 = self.uview(1)
                tf, rtf = self.TF[c % 2], self.r_tf[c % 2]
                sc.op("act", lambda e: e.activation(out=self.v3(tf[:, 0:TT]), in_=u1, func=AF.Sigmoid), rb1, [rtf])
                sc.op("dve", lambda e: e.tensor_tensor(out=self.v3(self.UD[:, c, HD:HD + TT]), in0=u0,
                                                       in1=self.v3(tf[:, 0:TT]), op=ALU.mult), rb0 + [rtf], [self.r_ud[c]])
                sc.op("dve", lambda e: e.tensor_tensor(out=self.UDL[:, c, 0:34], in0=self.psA[:, 2, 306:340],
                                                       in1=tf[:, 994:1028], op=ALU.mult), rb0 + [rtf], [self.r_udl])
            items.append(([ld], fnD))
        self.run_items(items)
        sc.dma("sp", [(self.ex_view(self.exs_h, 0, 8, HA), self.UA[:, :, HA + 1022:HA + 1024]),
                      (self.ex_view(self.exs_h, 2, 8, HC), self.UC[:, :, HC + 1009:HC + 1024]),
                      (self.ex_view(self.exs_h, 17, 8, HD), self.UD[:, :, HD + 994:HD + 1024])],
               self.r_ua + self.r_uc + self.r_ud, [self.r_exs_h])
        self.state_out(self.UAL, 6, [(self.ncap[l], 0, 2), (self.ncas[l], 4, 6)])
        self.state_out(self.UCL, 19, [(self.npcp[l], 0, 15), (self.npcs[l, 11:15, :], 15, 19)])
        self.state_out(self.UDL, 34, [(self.ncdp[l], 0, 30), (self.ncds[l, 26:30, :], 30, 34)])
        sc.op("pool", lambda e: e.tensor_copy(out=self.ESA[:, :, HA:HA + 4], in_=self.UAL[:, :, 2:6]), [self.r_ual], [self.r_esa])
        sc.op("pool", lambda e: e.tensor_copy(out=self.ESC[:, :, HC:HC + 4], in_=self.UCL[:, :, 15:19]), [self.r_ucl], [self.r_esc])
        sc.op("pool", lambda e: e.tensor_copy(out=self.ESD[:, :, HD:HD + 4], in_=self.UDL[:, :, 30:34]), [self.r_udl], [self.r_esd])

        def wb(col):
            return mkap(par[:, col:col + 1], [[1, 8], [0, 4]])
        for (ES, res_es, CS, res_cs, ntap, pcol, bias) in ((self.ESA, self.r_esa, self.CSA, self.r_csa, 3, P_AW, None),
                                                          (self.ESD, self.r_esd, self.CSD, self.r_csd, 31, P_DW, P_DB)):
            for j in range(ntap):
                if j == 0:
                    sc.op("pool", lambda e: e.tensor_tensor(out=CS[:, :, :], in0=ES[:, :, 0:4], in1=wb(pcol), op=ALU.mult),
                          [res_es, self.r_const], [res_cs])
                else:
                    sc.op("pool", lambda e, j=j: e.tensor_tensor(out=self.CST[:, :, :], in0=ES[:, :, j:j + 4], in1=wb(pcol + j * 8),
                                                              op=ALU.mult), [res_es, self.r_const], [self.r_cst])
                    sc.op("pool", lambda e: e.tensor_tensor(out=CS[:, :, :], in0=CS[:, :, :], in1=self.CST[:, :, :], op=ALU.add),
                          [self.r_cst], [res_cs])
            if bias is not None:
                sc.op("pool", lambda e: e.tensor_tensor(out=CS[:, :, :], in0=CS[:, :, :], in1=wb(bias), op=ALU.add),
                      [self.r_const], [res_cs])
        E = self.ESC
        L = self.PL
        n = HC + 4
        steps = ((1, E, L[0]), (2, L[0], L[1]), (4, L[1], L[0]), (8, L[0], L[1]))
        for g, (sh, src, dst) in enumerate(steps):
            lo = 2 * sh - 1
            rsrc = self.r_esc if src is E else self.r_pl[0 if src is L[0] else 1]
            rdst = self.r_pl[0 if dst is L[0] else 1]
            sc.op("dve", lambda e: e.tensor_tensor(out=dst[:, :, lo:n], in0=src[:, :, lo:n], in1=src[:, :, lo - sh:n - sh], op=ALU.add),
                  [rsrc], [rdst])
            w = 2 * sh
            sc.op("dve", lambda e: e.scalar_tensor_tensor(out=self.PSS[:, 2 * g:2 * g + 2, :], in0=dst[:, 2 * g:2 * g + 2, HC:HC + 4],
                                                          scalar=1.0 / w, in1=E[:, 2 * g:2 * g + 2, HC:HC + 4],
                                                          op0=ALU.mult, op1=ALU.subtract), [rdst, self.r_esc], [self.r_pss])

    def collective(self, src, dst, rsrc, rdst):
        sc = self.sc
        sc.deps("pool", [rsrc], [rdst])
        groups = [[2 * i, 2 * i + 1] for i in range(KCORES // 2)]
        cs = sc.ccsem
        self.nc.gpsimd.collective_compute("AllGather", ALU.bypass, replica_groups=groups,
                                          ins=[src], outs=[dst]).then_inc(cs.h, 1)
        cs.val += 1
        sc._record((cs, cs.val), [rsrc], [rdst])

    def halos(self, l):
        sc = self.sc
        for (U, ru, row0, H, o) in ((self.UA, self.r_ua, 0, HA, 0), (self.UC, self.r_uc, 2, HC, 2),
                                    (self.UD, self.r_ud, 17, HD, 17)):
            sc.dma("sp", [(self.HALO[:, :, o:o + H], self.ex_view(self.exd_h, row0, 8, H))], [self.r_exd_h], [self.r_halo])
            sc.op("dve", lambda e: e.tensor_scalar(out=U[:, :, 0:H], in0=self.HALO[:, :, o:o + H], scalar1=self.FLAG[:, 0:1],
                                                   scalar2=None, op0=ALU.mult), [self.r_halo, self.r_const], ru)

    def phase2A(self, l):
        sc = self.sc
        self.tphase("p2")
        par = self.PAR[l]
        items = []
        for c in range(8):
            ld = (lambda c=c: self.load_slab([(self.wi(l, O_BA + c * 128, 128), 0), (self.wi(l, O_ZA + c * 128, 128), 128)], NKC))

            def fn(slots, c=c):
                slot = slots[0]
                self.fm_matmuls(0, slot, 0, NKC, self.xnt_rhs)
                self.fm_matmuls(1, slot, 1, NKC, self.xnt_rhs)
                u0, rb0 = self.uview(0)
                u1, rb1 = self.uview(1)
                t3, rt3 = self.TF[(c % 2) * 2], self.r_tf[(c % 2) * 2]
                sz, rsz = self.TF[(c % 2) * 2 + 1], self.r_tf[(c % 2) * 2 + 1]
                ua = self.UA
                wc = [par[:, P_AW + j * 8 + c:P_AW + j * 8 + c + 1] for j in range(3)]
                sc.op("dve", lambda e: e.tensor_scalar(out=t3[:, 0:TT], in0=ua[:, c, 0:TT], scalar1=wc[0], scalar2=None,
                                                       op0=ALU.mult), [self.r_ua[c], self.r_const], [rt3])
                for j in (1, 2):
                    sc.op("dve", lambda e, j=j: e.scalar_tensor_tensor(out=t3[:, 0:TT], in0=ua[:, c, j:j + TT], scalar=wc[j],
                                                                     in1=t3[:, 0:TT], op0=ALU.mult, op1=ALU.add),
                          [self.r_ua[c], self.r_const], [rt3])
                sc.op("pool", lambda e: e.tensor_copy(out=t3[:, 1024:1028], in_=self.CSA[:, c, :]), [self.r_csa], [rt3])
                sc.op("act", lambda e: e.activation(out=self.v3(sz[:, 0:TT]), in_=u1, func=AF.Silu), rb1, [rsz])
                sc.op("dve", lambda e: e.tensor_tensor(out=self.v3(t3[:, 0:TT]), in0=u0, in1=self.v3(t3[:, 0:TT]), op=ALU.mult),
                      rb0, [rt3])
                sc.op("dve", lambda e: e.tensor_tensor(out=ua[:, c, HA:HA + TT], in0=t3[:, 0:TT], in1=sz[:, 0:TT], op=ALU.mult),
                      [rt3, rsz], [self.r_ua[c]])
            items.append(([ld], fn))
        self.run_items(items)

    def phase2C(self, l):
        sc = self.sc
        self.tphase("p2")
        par = self.PAR[l]
        items = []
        for g in range(4):
            ldp = (lambda g=g: self.load_slab([(self.w_pool[l, g], 0)], 2))
            ldz = (lambda g=g: self.load_slab([(self.wi(l, O_ZC + g * 256, 256), 0)], NKC))

            def fn(slots, g=g):
                sp_, sz_ = slots
                w = 2 ** (g + 1)
                n = HC + TT
                for j in range(2):
                    c = 2 * g + j
                    ext = self.UC[:, c, :]
                    la, lb = self.TF[0], self.TF[1]
                    ra, rb_ = self.r_tf[0], self.r_tf[1]
                    src, rsrc = ext, self.r_uc[c]
                    dst, rdst = la, ra
                    sh = 1
                    for lev in range(g + 1):
                        lo = 2 * sh - 1
                        sc.op("dve", lambda e, src=src, dst=dst, lo=lo, sh=sh: e.tensor_tensor(
                            out=dst[:, lo:n], in0=src[:, lo:n], in1=src[:, lo - sh:n - sh], op=ALU.add), [rsrc], [rdst])
                        src, rsrc = dst, rdst
                        dst, rdst = (lb, rb_) if dst is la else (la, ra)
                        sh *= 2
                    lf, rlf = src, rsrc
                    pt, rpt = self.TB[j], self.r_tb[j]
                    sc.op("dve", lambda e: e.scalar_tensor_tensor(out=pt[:, 0:TT], in0=lf[:, HC:HC + TT], scalar=1.0 / w,
                                                                  in1=ext[:, HC:HC + TT], op0=ALU.mult, op1=ALU.subtract),
                          [rlf, self.r_uc[c]], [rpt])
                    tmp = self.SMALL[:, 32:48]
                    sc.op("dve", lambda e: e.tensor_tensor(out=tmp, in0=lf[:, HC:HC + 16], in1=self.INVC[:, g, :], op=ALU.mult),
                          [rlf, self.r_const], [self.r_small])
                    sc.op("dve", lambda e: e.tensor_tensor(out=pt[:, 0:16], in0=tmp, in1=ext[:, HC:HC + 16], op=ALU.subtract),
                          [self.r_small, self.r_uc[c]], [rpt])
                    sc.op("pool", lambda e: e.tensor_copy(out=pt[:, 1024:1028], in_=self.PSS[:, c, :]), [self.r_pss], [rpt])
                for j in range(2):
                    c = 2 * g + j
                    self.fm_matmuls(0, sp_, j, 2, lambda kc, gi: (self.TB[kc][:, gi * GW:(gi + 1) * GW], [self.r_tb[kc]]))
                    self.fm_matmuls(1, sz_, j, NKC, self.xnt_rhs)
                    u0, rb0 = self.uview(0)
                    u1, rb1 = self.uview(1)
                    sz, rsz = self.TF[2 + j], self.r_tf[2 + j]
                    sc.op("act", lambda e: e.activation(out=self.v3(sz[:, 0:TT]), in_=u1, func=AF.Silu), rb1, [rsz])
                    sc.op("dve", lambda e: e.scalar_tensor_tensor(out=self.v3(self.UC[:, c, HC:HC + TT]), in0=u0,
                                                                  scalar=par[:, P_CS + c:P_CS + c + 1], in1=self.v3(sz[:, 0:TT]),
                                                                  op0=ALU.mult, op1=ALU.mult),
                          rb0 + [rsz, self.r_const], [self.r_uc[c]])
            items.append(([ldp, ldz], fn))
        self.run_items(items)

    def phase2D(self, l):
        sc = self.sc
        self.tphase("p2")
        par = self.PAR[l]
        items = []
        cntu = {"n": 0}
        for c in range(8):
            def ldg(c=c):
                slot = self.nslab % NSLOT
                self.nslab += 1
                ws = self.WS[slot]
                sc.deps("pool", [], [self.r_ws[slot]])
                for j in range(31):
                    dst_ = ws[:, j // 2, (j % 2) * 128:(j % 2) * 128 + 128]
                    wcol = par[:, P_DW + j * 8 + c:P_DW + j * 8 + c + 1]
                    if j % 2 == 0:
                        sc.op("dve", lambda e, dst_=dst_, wcol=wcol: e.tensor_scalar(out=dst_, in0=self.identb[:, :], scalar1=wcol, scalar2=None,
                                                                                   op0=ALU.mult), [self.r_const], [self.r_ws[slot]])
                    else:
                        sc.op("act", lambda e, dst_=dst_, wcol=wcol: e.activation(out=dst_, in_=self.identb[:, :], func=AF.Copy, scale=wcol),
                              [self.r_const], [self.r_ws[slot]])
                return slot

            def fn(slots, c=c):
                slot = slots[0]
                ws = self.WS[slot]
                u = cntu["n"] % 2
                cntu["n"] += 1
                ps, rb = self.unit(u)
                for j in range(31):
                    for gi in range(3):
                        sc.op("pe", lambda e, j=j, gi=gi: e.matmul(
                            ps[:, gi, 0:GW], lhsT=ws[:, j // 2, (j % 2) * 128:(j % 2) * 128 + 128],
                            rhs=self.UD[:, c, j + gi * GW:j + gi * GW + GW], start=(j == 0), stop=(j == 30)),
                            [self.r_ws[slot], self.r_ud[c]], [rb[gi]], signal=(j == 30 and gi == 2))
                sc.op("act", lambda e: e.activation(out=self.v3(self.UD[:, c, HD:HD + TT]), in_=ps[:, :, 0:GW], func=AF.Identity,
                                                    bias=par[:, P_DB + c:P_DB + c + 1], scale=1.0), rb + [self.r_const], [self.r_ud[c]])
                sc.op("pool", lambda e: e.tensor_copy(out=self.UD[:, c, HD + 1024:HD + 1028], in_=self.CSD[:, c, :]),
                      [self.r_csd], [self.r_ud[c]])
            items.append(([ldg], fn))
        self.run_items(items)
        pm, rbm = self.unit(0)
        pe2, rbe = self.unit(1)
        for c in range(8):
            sq, rsq = self.TB[c % 2], self.r_tb[c % 2]
            sc.op("act", lambda e: e.activation(out=sq[:, 0:TT], in_=self.UD[:, c, HD:HD + TT], func=AF.Square),
                  [self.r_ud[c]], [rsq])
            for gi in range(3):
                sc.op("pe", lambda e, gi=gi: e.matmul(pm[:, gi, 0:GW], lhsT=self.onesm[:, :],
                                                    rhs=self.UD[:, c, HD + gi * GW:HD + (gi + 1) * GW], start=(c == 0), stop=(c == 7)),
                      [self.r_const, self.r_ud[c]], [rbm[gi]], signal=True)
                sc.op("pe", lambda e, gi=gi: e.matmul(pe2[:, gi, 0:GW], lhsT=self.onesm[:, :],
                                                    rhs=sq[:, gi * GW:(gi + 1) * GW], start=(c == 0), stop=(c == 7)),
                      [self.r_const, rsq], [rbe[gi]], signal=True)
        mean, rmean = self.TF[0], self.r_tf[0]
        rstd, rrstd = self.TF[1], self.r_tf[1]
        sc.op("act", lambda e: e.activation(out=self.v3(mean[:, 0:TT]), in_=pm[:, :, 0:GW], func=AF.Copy), rbm, [rmean])
        sc.op("dve", lambda e: e.tensor_tensor(out=rstd[:, 0:TT], in0=mean[:, 0:TT], in1=mean[:, 0:TT], op=ALU.mult), [rmean], [rrstd])
        sc.op("dve", lambda e: e.tensor_tensor(out=self.v3(rstd[:, 0:TT]), in0=pe2[:, :, 0:GW], in1=self.v3(rstd[:, 0:TT]),
                                               op=ALU.subtract), rbe, [rrstd])
        sc.op("dve", lambda e: e.tensor_scalar(out=rstd[:, 0:TT], in0=rstd[:, 0:TT], scalar1=0.0, scalar2=EPS, op0=ALU.max, op1=ALU.add),
              [], [rrstd])
        sc.op("act", lambda e: e.activation(out=rstd[:, 0:TT], in_=rstd[:, 0:TT], func=AF.Sqrt), [], [rrstd])
        sc.op("dve", lambda e: e.reciprocal(out=rstd[:, 0:TT], in_=rstd[:, 0:TT]), [], [rrstd])
        items = []
        for cp in range(4):
            ld = (lambda cp=cp: self.load_slab([(self.wi(l, O_ZD + cp * 256, 256), 0)], NKC))

            def fn2(slots, cp=cp):
                slot = slots[0]
                for j in range(2):
                    c = 2 * cp + j
                    self.fm_matmuls(j, slot, j, NKC, self.xnt_rhs)
                    u, rb = self.uview(j)
                    sz, rsz = self.TF[2], self.r_tf[2]
                    t, rt = self.TF[3], self.r_tf[3]
                    ud = self.UD[:, c, HD:HD + TT]
                    sc.op("dve", lambda e: e.tensor_tensor(out=t[:, 0:TT], in0=ud, in1=mean[:, 0:TT], op=ALU.subtract),
                          [self.r_ud[c], rmean], [rt])
                    sc.op("dve", lambda e: e.tensor_tensor(out=t[:, 0:TT], in0=t[:, 0:TT], in1=rstd[:, 0:TT], op=ALU.mult),
                          [rrstd], [rt])
                    sc.op("act", lambda e: e.activation(out=t[:, 0:TT], in_=t[:, 0:TT], func=AF.Silu,
                                                        bias=par[:, P_LB + c:P_LB + c + 1], scale=par[:, P_LG + c:P_LG + c + 1]),
                          [self.r_const], [rt])
                    sc.op("act", lambda e: e.activation(out=self.v3(sz[:, 0:TT]), in_=u, func=AF.Silu), rb, [rsz])
                    sc.op("dve", lambda e: e.tensor_tensor(out=ud, in0=t[:, 0:TT], in1=sz[:, 0:TT], op=ALU.mult),
                          [rt, rsz], [self.r_ud[c]])
            items.append(([ld], fn2))
        self.run_items(items)

    def phase2B(self, l):
        sc = self.sc
        self.tphase("p2")
        self.sc.alias(self.attn_res, self.r_merged)
        nc = self.nc
        SCALE = 128.0 ** -0.5
        psT2 = self.psA[:].bitcast(BF16)[:, 2, :]
        rb2 = [self.r_bank[2]]
        items = []

        def kv_ap(t, row0, h, dims):
            return bass.AP(t.tensor, row0 * 1024 + h * 128, dims)

        def head(slots, h):
            sA, sB = slots
            wa, wb_ = self.WS[sA], self.WS[sB]
            pending = [None]
            for i, tok0, rows in self.tiles():
                b = i % 2
                pb = self.bank(b)
                rb = [self.r_bank[b]]
                for kc in range(NKC):
                    sc.op("pe", lambda e, kc=kc: e.matmul(pb[0:rows, 0:256], lhsT=self.XNT[:, kc, tok0:tok0 + rows], rhs=wa[:, kc, 0:256],
                                                        start=(kc == 0), stop=(kc == NKC - 1)),
                          [self.r_xnt, self.r_ws[sA]], rb, signal=False)
                for kc in range(NKC):
                    sc.op("pe", lambda e, kc=kc: e.matmul(pb[0:rows, 256:384], lhsT=self.XNT[:, kc, tok0:tok0 + rows], rhs=wb_[:, kc, 0:128],
                                                        start=(kc == 0), stop=(kc == NKC - 1)),
                          [self.r_xnt, self.r_ws[sB]], rb, signal=(kc == NKC - 1))
                qf, rqf = self.QF[i % 2], self.r_qf[i % 2]
                qb, rqb = self.QB[i % 2], self.r_qb[i % 2]
                if pending[0] is not None:
                    pending[0]()
                    pending[0] = None
                self.norm_rope(pb[0:rows, 0:384], rows, 3, self.GQ[l], i, qf[0:rows, :, :], rqf, rb)
                sc.op("dve", lambda e: e.tensor_copy(out=qb[0:rows, :, :], in_=qf[0:rows, :, :]), [rqf], [rqb])

                def later(i=i, tok0=tok0, rows=rows, qb=qb, rqb=rqb):
                    for g in range(3):
                        sc.op("pe", lambda e, g=g: e.transpose(out=psT2[:, g * 128:g * 128 + rows], in_=qb[0:rows, g, :],
                                                              identity=self.identb[0:rows, 0:rows]), [rqb, self.r_const], rb2)
                    qtn, rqtn = self.QTN[i % 2], self.r_qtn[i % 2]
                    sc.op("act", lambda e: e.activation(out=qtn[:, 0:384], in_=psT2[:, 0:384], func=AF.Copy), rb2, [rqtn])
                    if i < 8:
                        sc.op("pool", lambda e: e.tensor_copy(out=self.QT[:, 0, tok0:tok0 + 128], in_=qtn[:, 0:128]), [rqtn], [self.r_qt[0]])
                        d1 = self.QT[:, 1, 0:1024].rearrange("c (r j) -> c r j", r=4)[:, :, i * 32:(i + 1) * 32]
                        sc.op("pool", lambda e: e.tensor_copy(out=d1, in_=qtn[:, 128:256].rearrange("c (m r) -> c r m", r=4)),
                              [rqtn], [self.r_qt[1]])
                        d2 = self.QT[:, 2, 0:1024].rearrange("c (r j) -> c r j", r=16)[:, :, i * 8:(i + 1) * 8]
                        sc.op("pool", lambda e: e.tensor_copy(out=d2, in_=qtn[:, 256:384].rearrange("c (m r) -> c r m", r=16)),
                              [rqtn], [self.r_qt[2]])
                    else:
                        sc.op("pool", lambda e: e.tensor_copy(out=self.QS[:, 0:12].rearrange("c (g s) -> c g s", g=3),
                                                              in_=qtn[:, 0:384].rearrange("c (g m) -> c g m", g=3)[:, :, 0:4]),
                              [rqtn], [self.r_qs])

                pending[0] = later
            if pending[0] is not None:
                pending[0]()

            if KDBG & 16:
                return

            def load_tiles(g, which):
                for (dst, rdst, exd_, exs_, rxd, rxs, nm) in ((self.KC, self.r_kc, self.exd_k, self.exs_k, self.r_exd_k, self.r_exs_k, "k"),
                                                              (self.VC, self.r_vc, self.exd_v, self.exs_v, self.r_exd_v, self.r_exs_v, "v")):
                    if nm != which:
                        continue
                    pairs = []
                    r0 = 0
                    if g == 0:
                        pairs.append((dst[:, 0, :], kv_ap(exd_, r0 + 896, h, [[1024, 128], [1, 128]])))
                        pairs.append((dst[:, 1:9, :], kv_ap(exs_, r0, h, [[1024, 128], [128 * 1024, 8], [1, 128]])))
                    elif g == 1:
                        pairs.append((dst[:, 0:12:3, :], kv_ap(exd_, r0 + 512, h, [[4096, 128], [1024, 4], [1, 128]])))
                        pairs.append((dst[:, 1:12:3, :], kv_ap(exs_, r0, h, [[4096, 128], [1024, 4], [1, 128]])))
                        pairs.append((dst[:, 2:12:3, :], kv_ap(exs_, r0 + 512, h, [[4096, 128], [1024, 4], [1, 128]])))
                    else:
                        pairs.append((dst[0:64, 0:16, :], kv_ap(exd_, r0, h, [[16384, 64], [1024, 16], [1, 128]])))
                        pairs.append((dst[64:128, 0:16, :], kv_ap(exs_, r0, h, [[16384, 64], [1024, 16], [1, 128]])))
                    sc.dma("sp", pairs, [rxd, rxs], [rdst])
                return (9, 12, 16)[g]

            def load_sample_k():
                sc.dma("pool", [(self.KC[:, 0:16, :], bass.AP(self.ck.tensor, l * 2048 * 1024 + h * 128, [[1024, 128], [128 * 1024, 16], [1, 128]]))],
                       [], [self.r_kc])

            def transpose_k(ntile):
                t0 = 0
                while t0 < ntile:
                    nb = min(8, ntile - t0)
                    for k in range(nb):
                        sc.op("pe", lambda e, k=k: e.transpose(out=psT2[:, k * 128:(k + 1) * 128], in_=self.KC[:, t0 + k, :],
                                                              identity=self.identb[:, :]), [self.r_kc, self.r_const], rb2)
                    sc.op("act", lambda e: e.activation(out=self.KT[:, t0:t0 + nb, :],
                                                        in_=psT2[:, 0:nb * 128].rearrange("c (t k) -> c t k", t=nb), func=AF.Copy),
                          rb2, [self.r_kt])
                    t0 += nb

            O, rO = self.bank(5), [self.r_bank[5]]
            Dn, rD = self.bank(6), [self.r_bank[6]]
            nS = {"n": 0}

            def evac(g, hf):
                if g == 0:
                    nv = self.NUM[:, hf * 512:(hf + 1) * 512]
                    dv = self.DEN[:, hf * 512:(hf + 1) * 512]
                    ov, dnv = O[:, 0:512], Dn[:, 0:512]
                    sc.op("act", lambda e: e.activation(out=nv, in_=ov, func=AF.Copy), rO, [self.r_num])
                    sc.op("act", lambda e: e.activation(out=dv, in_=dnv, func=AF.Copy), rD, [self.r_den])
                    return
                if g == 1:
                    nv = self.NUM[:, 0:1024].rearrange("c (j r) -> c r j", r=4)[:, 2 * hf:2 * hf + 2, :]
                    dv = self.DEN[:, 0:1024].rearrange("c (j r) -> c r j", r=4)[:, 2 * hf:2 * hf + 2, :]
                    ov = O[:, 0:512].rearrange("c (r j) -> c r j", r=2)
                    dnv = Dn[:, 0:512].rearrange("c (r j) -> c r j", r=2)
                else:
                    nv = self.NUM[:, 0:1024].rearrange("c (j r) -> c r j", r=16)[:, 8 * hf:8 * hf + 8, :]
                    dv = self.DEN[:, 0:1024].rearrange("c (j r) -> c r j", r=16)[:, 8 * hf:8 * hf + 8, :]
                    ov = O[:, 0:512].rearrange("c (r j) -> c r j", r=8)
                    dnv = Dn[:, 0:512].rearrange("c (r j) -> c r j", r=8)
                sc.op("dve", lambda e: e.tensor_tensor(out=nv, in0=ov, in1=nv, op=ALU.add), rO, [self.r_num])
                sc.op("dve", lambda e: e.tensor_tensor(out=dv, in0=dnv, in1=dv, op=ALU.add), rD, [self.r_den])

            def job_parts(g, ip, isame, qcols, mask, slot, n):
                Sb = self.bank(3 + n % 2)
                rS = [self.r_bank[3 + n % 2]]
                pt, rpt = self.PT[n % 2], self.r_pt[n % 2]
                oc = O[:, slot * 128:(slot + 1) * 128]
                dc = Dn[:, slot * 128:(slot + 1) * 128]

                def partA():
                    sc.op("pe", lambda e: e.matmul(Sb[:, 0:128], lhsT=self.KT[:, ip, :], rhs=qcols, start=True, stop=True),
                          [self.r_kt, self.r_qt[g]], rS, signal=False)
                    sc.op("pe", lambda e: e.matmul(Sb[:, 128:256], lhsT=self.KT[:, isame, :], rhs=qcols, start=True, stop=True),
                          [self.r_kt, self.r_qt[g]], rS)
                    sc.op("act", lambda e: e.activation(out=pt[:, :], in_=Sb[:, 0:256], func=AF.Exp, scale=SCALE), rS, [rpt])
                    sc.op("dve", lambda e: e.tensor_tensor(out=pt[:, :], in0=pt[:, :], in1=mask[:, :], op=ALU.mult), [self.r_const], [rpt])

                def partB():
                    sc.op("pe", lambda e: e.matmul(oc, lhsT=self.VC[:, ip, :], rhs=pt[:, 0:128], start=True, stop=False),
                          [self.r_vc, rpt], rO, signal=False)
                    sc.op("pe", lambda e: e.matmul(oc, lhsT=self.VC[:, isame, :], rhs=pt[:, 128:256], start=False, stop=True),
                          [self.r_vc, rpt], rO, signal=False)
                    sc.op("pe", lambda e: e.matmul(dc, lhsT=self.onesb[:, :], rhs=pt[:, 0:128], start=True, stop=False),
                          [self.r_const, rpt], rD, signal=False)
                    sc.op("pe", lambda e: e.matmul(dc, lhsT=self.onesb[:, :], rhs=pt[:, 128:256], start=False, stop=True),
                          [self.r_const, rpt], rD)
                return partA, partB

            def run_pipe(parts):
                prevB = None
                for (pa, pb_) in parts:
                    pa()
                    if prevB is not None:
                        prevB()
                    prevB = pb_
                if prevB is not None:
                    prevB()

            nt = load_tiles(0, "k")
            load_tiles(0, "v")
            transpose_k(nt)
            load_tiles(1, "k")
            parts = []
            for qt in range(8):
                pa, pb_ = job_parts(0, qt, qt + 1, self.QT[:, 0, qt * 128:(qt + 1) * 128], self.M1F if qt == 0 else self.M1, qt % 4, qt)
                if qt % 4 == 3:
                    pb_ = (lambda pb_=pb_, hf=qt // 4: (pb_(), evac(0, hf)))
                parts.append((pa, pb_))
            run_pipe(parts)
            if KDBG & 32:
                return
            nt = 12
            load_tiles(1, "v")
            transpose_k(nt)
            load_tiles(2, "k")
            parts = []
            k = 0
            for r in range(4):
                for j in range(2):
                    pa, pb_ = job_parts(1, r * 3 + j, r * 3 + j + 1, self.QT[:, 1, r * 256 + j * 128:r * 256 + (j + 1) * 128],
                                        self.M1F if j == 0 else self.M1, k % 4, k)
                    if k % 4 == 3:
                        pb_ = (lambda pb_=pb_, hf=k // 4: (pb_(), evac(1, hf)))
                    parts.append((pa, pb_))
                    k += 1
            run_pipe(parts)
            nt = 16
            load_tiles(2, "v")
            transpose_k(nt)
            load_sample_k()
            parts = []
            for r4 in range(4):
                def mk(r4=r4):
                    n = r4
                    Sb = self.bank(3 + n % 2)
                    rS = [self.r_bank[3 + n % 2]]
                    pt, rpt = self.PT[n % 2], self.r_pt[n % 2]

                    def partA():
                        for k4 in range(4):
                            r = r4 * 4 + k4
                            sc.op("pe", lambda e, r=r, k4=k4: e.matmul(Sb[:, k4 * 64:(k4 + 1) * 64], lhsT=self.KT[:, r, :],
                                                                     rhs=self.QT[:, 2, r * 64:(r + 1) * 64], start=True, stop=True),
                                  [self.r_kt, self.r_qt[2]], rS, signal=(k4 == 3))
                        sc.op("act", lambda e: e.activation(out=pt[:, :], in_=Sb[:, 0:256], func=AF.Exp, scale=SCALE), rS, [rpt])
                        ptv = pt[:, :].rearrange("k (a q) -> k a q", a=4)
                        sc.op("dve", lambda e: e.tensor_tensor(out=ptv, in0=ptv, in1=mkap(self.M16[:, 0:1], [[0, 4], [1, 64]]), op=ALU.mult),
                              [self.r_const], [rpt])

                    def partB():
                        for k4 in range(4):
                            r = r4 * 4 + k4
                            sl = r % 8
                            sc.op("pe", lambda e, r=r, k4=k4, sl=sl: e.matmul(O[:, sl * 64:(sl + 1) * 64], lhsT=self.VC[:, r, :],
                                                                            rhs=pt[:, k4 * 64:(k4 + 1) * 64], start=True, stop=True),
                                  [self.r_vc, rpt], rO, signal=False)
                            sc.op("pe", lambda e, k4=k4, sl=sl: e.matmul(Dn[:, sl * 64:(sl + 1) * 64], lhsT=self.onesb[:, :],
                                                                      rhs=pt[:, k4 * 64:(k4 + 1) * 64], start=True, stop=True),
                                  [self.r_const, rpt], rD, signal=(k4 == 3))
                        if r4 % 2 == 1:
                            evac(2, r4 // 2)
                    return partA, partB
                parts.append(mk())
            run_pipe(parts)

            if KDBG & 64:
                return
            sc.dma("pool", [(self.VC[:, 0:16, :], bass.AP(self.cv.tensor, l * 2048 * 1024 + h * 128, [[1024, 128], [128 * 1024, 16], [1, 128]]))],
                   [], [self.r_vc])
            transpose_k(16)
            sc.op("pe", lambda e: e.transpose(out=psT2[:, 0:4], in_=self.KSB[0:4, h * 128:(h + 1) * 128], identity=self.identb[0:4, 0:4]),
                  [self.r_ksb, self.r_const], rb2)
            sc.op("act", lambda e: e.activation(out=self.KTN[:, 0:4], in_=psT2[:, 0:4], func=AF.Copy), rb2, [self.r_ktn])
            b7, r7 = self.bank(7), [self.r_bank[7]]
            for t in range(16):
                sc.op("pe", lambda e, t=t: e.matmul(b7[:, t * 12:(t + 1) * 12], lhsT=self.KT[:, t, :], rhs=self.QS[:, 0:12], start=True, stop=True),
                      [self.r_kt, self.r_qs], r7, signal=False)
            sc.op("pe", lambda e: e.matmul(b7[0:4, 192:204], lhsT=self.KTN[:, 0:4], rhs=self.QS[:, 0:12], start=True, stop=True),
                  [self.r_ktn, self.r_qs], r7)
            sc.op("act", lambda e: e.activation(out=self.PTS[:, 0:16, :], in_=b7[:, 0:192].rearrange("k (t q) -> k t q", t=16),
                                                func=AF.Exp, scale=SCALE), r7, [self.r_pts])
            sc.op("act", lambda e: e.activation(out=self.PTS[0:4, 16, :], in_=b7[0:4, 192:204], func=AF.Exp, scale=SCALE), r7, [self.r_pts])
            sc.op("dve", lambda e: e.tensor_tensor(out=self.PTS[:, 0:16, :], in0=self.PTS[:, 0:16, :], in1=self.MS[:, 0:16, :], op=ALU.mult),
                  [self.r_const], [self.r_pts])
            sc.op("dve", lambda e: e.tensor_tensor(out=self.PTS[0:4, 16, :], in0=self.PTS[0:4, 16, :], in1=self.MS[0:4, 16, :], op=ALU.mult),
                  [self.r_const], [self.r_pts])
            for t in range(16):
                sc.op("pe", lambda e, t=t: e.matmul(O[:, 0:12], lhsT=self.VC[:, t, :], rhs=self.PTS[:, t, :], start=(t == 0), stop=False),
                      [self.r_vc, self.r_pts], rO, signal=False)
            sc.op("pe", lambda e: e.matmul(O[:, 0:12], lhsT=self.VSB[0:4, h * 128:(h + 1) * 128], rhs=self.PTS[0:4, 16, :], start=False, stop=True),
                  [self.r_vsb, self.r_pts], rO)
            for t in range(16):
                sc.op("pe", lambda e, t=t: e.matmul(Dn[:, 0:12], lhsT=self.onesb[:, :], rhs=self.PTS[:, t, :], start=(t == 0), stop=False),
                      [self.r_const, self.r_pts], rD, signal=False)
            sc.op("pe", lambda e: e.matmul(Dn[:, 0:12], lhsT=self.onesb[0:4, :], rhs=self.PTS[0:4, 16, :], start=False, stop=True),
                  [self.r_const, self.r_pts], rD)
            sc.op("act", lambda e: e.activation(out=self.OSS[:, 0, :], in_=O[:, 0:12], func=AF.Copy), rO, [self.r_oss])
            sc.op("act", lambda e: e.activation(out=self.OSS[:, 1, :], in_=Dn[:, 0:12], func=AF.Copy), rD, [self.r_oss])
            for (k_, dstt, rd) in ((0, self.NUM, self.r_num), (1, self.DEN, self.r_den)):
                sc.op("dve", lambda e: e.tensor_tensor(out=dstt[:, 1024:1028], in0=self.OSS[:, k_, 0:4], in1=self.OSS[:, k_, 4:8], op=ALU.add),
                      [self.r_oss], [rd])
                sc.op("dve", lambda e: e.tensor_tensor(out=dstt[:, 1024:1028], in0=dstt[:, 1024:1028], in1=self.OSS[:, k_, 8:12], op=ALU.add),
                      [self.r_oss], [rd])

            self.fm_matmuls(0, sB, 1, NKC, self.xnt_rhs)
            u0, rb0 = self.uview(0)
            sz, rsz = self.TF[2], self.r_tf[2]
            sc.op("act", lambda e: e.activation(out=self.v3(sz[:, 0:TT]), in_=u0, func=AF.Silu), rb0, [rsz])
            sc.op("dve", lambda e: e.reciprocal(out=self.DEN[:, :], in_=self.DEN[:, :]), [], [self.r_den])
            sc.op("dve", lambda e: e.tensor_tensor(out=self.NUM[:, :], in0=self.NUM[:, :], in1=self.DEN[:, :], op=ALU.mult),
                  [self.r_den], [self.r_num])
            sc.op("dve", lambda e: e.tensor_tensor(out=self.YB[:, h, :], in0=self.NUM[:, :], in1=sz[:, 0:TT], op=ALU.mult),
                  [self.r_num, rsz], [self.r_yb[h]])
            sc.op("dve", lambda e: e.memset(self.NUM[:, 1028:1032], 1.0), [], [self.r_num])
            sc.op("dve", lambda e: e.memset(self.DEN[:, 1028:1032], 1.0), [], [self.r_den])

        for h in range(8 if not (KDBG & 128) else 1):
            ldA = (lambda h=h: self.load_slab([(self.wi(l, O_Q + h * 128, 128), 0), (self.wi(l, O_Q + 1024 + h * 128, 128), 128)], NKC))
            ldB = (lambda h=h: self.load_slab([(self.wi(l, O_Q + 2048 + h * 128, 128), 0), (self.wi(l, O_ZB + h * 128, 128), 128)], NKC))
            items.append(([ldA, ldB], (lambda slots, h=h: head(slots, h))))
        self.run_items(items)

    def phase3(self, l):
        sc = self.sc
        self.tphase("p2")
        self.sc.alias(self.r_merged, self.attn_res)
        ystore = ((self.UA, self.r_ua, HA), (self.YB, self.r_yb, 0), (self.UC, self.r_uc, HC), (self.UD, self.r_ud, HD))
        items = []
        for dp in range(8):
            for i in range(4):
                ldg = (lambda dp=dp, i=i: self.load_slab([(self.wi(l, O_GT + i * 2048 + dp * 256, 256), 0)], NKC))
                ldp = (lambda dp=dp, i=i: self.load_slab([(self.w_br[l, i, :, dp * 256:(dp + 1) * 256], 0)], 8))

                def fn(slots, dp=dp, i=i):
                    sg, sp_ = slots
                    Y, rY, H = ystore[i]
                    for hh in range(2):
                        dc = dp * 2 + hh
                        self.fm_matmuls(0, sg, hh, NKC, self.xnt_rhs)
                        self.fm_matmuls(1, sp_, hh, 8, lambda kc, gi: (Y[:, kc, H + gi * GW:H + (gi + 1) * GW], [rY[kc]]))
                        u0, rb0 = self.uview(0)
                        u1, rb1 = self.uview(1)
                        gs, rgs = self.TB[hh], self.r_tb[hh]
                        acc, racc = self.TF[hh], self.r_tf[hh]
                        tmp, rtmp = self.TF[2 + hh], self.r_tf[2 + hh]
                        sc.op("act", lambda e: e.activation(out=self.v3(gs[:, 0:TT]), in_=u0, func=AF.Sigmoid), rb0, [rgs])
                        if i == 0:
                            sc.op("dve", lambda e: e.tensor_tensor(out=self.v3(acc[:, 0:TT]), in0=u1, in1=self.v3(gs[:, 0:TT]), op=ALU.mult),
                                  rb1 + [rgs], [racc])
                        else:
                            sc.op("dve", lambda e: e.tensor_tensor(out=self.v3(tmp[:, 0:TT]), in0=u1, in1=self.v3(gs[:, 0:TT]), op=ALU.mult),
                                  rb1 + [rgs], [rtmp])
                            if i < 3:
                                sc.op("dve", lambda e: e.tensor_tensor(out=acc[:, 0:TT], in0=acc[:, 0:TT], in1=tmp[:, 0:TT], op=ALU.add),
                                      [rtmp], [racc])
                            else:
                                sc.op("dve", lambda e: e.tensor_tensor(out=self.MERGED[:, dc, :], in0=acc[:, 0:TT], in1=tmp[:, 0:TT], op=ALU.add),
                                      [rtmp, racc], [self.r_merged[dc]])
                items.append(([ldg, ldp], fn))
        self.run_items(items)

    def phase4(self, l):
        sc = self.sc
        self.tphase("p4")
        items = []
        cnt = {"n": 0}
        last = (l == NL - 1) or STAGE < 99
        for s_ in range(4):
            lda = (lambda s_=s_: self.load_slab([(self.w_out[l, :, s_ * 512:s_ * 512 + 256], 0)], NKC))
            ldb = (lambda s_=s_: self.load_slab([(self.w_out[l, :, s_ * 512 + 256:(s_ + 1) * 512], 0)], NKC))

            def fn(slots, s_=s_):
                c0 = s_ * 512
                for i, tok0, rows in self.tiles():
                    n = cnt["n"]
                    cnt["n"] += 1
                    b = n % 4
                    pb = self.bank(b)
                    rb = [self.r_bank[b]]
                    xsl, rx = self.XSL[n % 2], self.r_xsl[n % 2]
                    osl, ro = self.OSL[n % 2], self.r_osl[n % 2]
                    if l == 0:
                        src = self.xp[tok0:tok0 + rows, c0:c0 + 512] if i < 8 else self.xs[:, c0:c0 + 512]
                        rsrc = []
                    else:
                        src = self.x1[tok0:tok0 + rows, c0:c0 + 512]
                        rsrc = [self.r_x1]
                    sc.dma("sp", [(xsl[0:rows, :], src)], rsrc, [rx])
                    for hf, slot in enumerate(slots):
                        ws = self.WS[slot]
                        for kc in range(NKC):
                            sc.op("pe", lambda e, kc=kc, ws=ws, hf=hf: e.matmul(
                                pb[0:rows, hf * 256:(hf + 1) * 256], lhsT=self.MERGED[:, kc, tok0:tok0 + rows], rhs=ws[:, kc, 0:256],
                                start=(kc == 0), stop=(kc == NKC - 1)),
                                [self.r_merged[kc], self.r_ws[slot]], rb, signal=(kc == NKC - 1 and hf == 1))
                    sc.op("dve", lambda e: e.tensor_tensor(out=osl[0:rows, :], in0=pb[0:rows, 0:512], in1=xsl[0:rows, :], op=ALU.add),
                          rb + [rx], [ro])
                    if last:
                        dst = self.yp[tok0:tok0 + rows, c0:c0 + 512] if i < 8 else self.ys[:, c0:c0 + 512]
                        sc.dma("sp", [(dst, osl[0:rows, :])], [ro], [self.out_res])
                    else:
                        sc.dma("sp", [(self.x1[tok0:tok0 + rows, c0:c0 + 512], osl[0:rows, :])], [ro], [self.r_x1])
            items.append(([lda, ldb], fn))
        self.run_items(items)

    def finish(self):
        sc = self.sc
        for sem in sc.dsem:
            if sem.val > 0:
                sc._wait("sp", (sem, sem.val))
        for k in ("pe", "act", "dve", "pool"):
            s = sc.sem[k]
            if s.val > 0:
                sc._wait("sp", (s, s.val))


def _rope_tables(base):
    half = 16
    inv = (np.float32(500000.0) ** (-(np.arange(half, dtype=np.float32) / np.float32(half)))).astype(np.float32)
    rc = np.zeros((128, 9, 32), np.float32)
    rs = np.zeros((128, 9, 32), np.float32)
    for i in range(9):
        if i < 8:
            pos = (base + i * 128 + np.arange(128)).astype(np.float32)
            n = 128
        else:
            pos = (PAST + np.arange(4)).astype(np.float32)
            n = 4
        ang = (pos[:, None] * inv[None, :]).astype(np.float32)
        c = np.cos(ang).astype(np.float32)
        s_ = np.sin(ang).astype(np.float32)
        rc[:n, i, 0:16] = c
        rc[:n, i, 16:32] = c
        rs[:n, i, 0:16] = -s_
        rs[:n, i, 16:32] = s_
    return rc, rs


def _masks(odd):
    bf = ml_dtypes.bfloat16
    k = np.arange(128)[:, None]
    q = np.arange(128)[None, :]
    m1 = np.zeros((128, 256), np.float32)
    m1[:, 0:128] = (q <= k)
    m1[:, 128:256] = (q >= k)
    m1f = m1.copy()
    if not odd:
        m1f[:, 0:128] = 0.0
    m16 = np.zeros((128, 64), np.float32)
    m16[0:64, :] = 1.0 if odd else 0.0
    kk = np.arange(64)[:, None]
    qq = np.arange(64)[None, :]
    m16[64:128, :] = (kk <= qq)
    ms = np.zeros((128, 17, 12), np.float32)
    for g, (w, d) in enumerate(((128, 1), (512, 4), (2048, 16))):
        for s_ in range(4):
            for tile in range(17):
                for p in range(128):
                    e = tile * 128 + p
                    if tile == 16 and p >= 4:
                        continue
                    dist = 2048 + s_ - e
                    if dist >= 0 and dist % d == 0 and dist // d <= w // d:
                        ms[p, tile, g * 4 + s_] = 1.0
    return m1.astype(bf), m1f.astype(bf), m16.astype(bf), ms.astype(bf)


def _params(inp, l):
    p = np.zeros((128, NPAR), np.float32)
    p[:, P_NG:P_NG + 16] = inp["norm_g"][l].reshape(16, 128).T
    aw = inp["a_conv_w"][l]
    for j in range(3):
        p[:, P_AW + j * 8:P_AW + j * 8 + 8] = aw[j].reshape(8, 128).T
    p[:, P_CS:P_CS + 8] = inp["c_scale"][l].reshape(8, 128).T
    dw = inp["d_conv_w"][l]
    for j in range(31):
        p[:, P_DW + j * 8:P_DW + j * 8 + 8] = dw[j].reshape(8, 128).T
    p[:, P_DB:P_DB + 8] = inp["d_conv_b"][l].reshape(8, 128).T
    p[:, P_LG:P_LG + 8] = inp["d_ln_g"][l].reshape(8, 128).T
    p[:, P_LB:P_LB + 8] = inp["d_ln_b"][l].reshape(8, 128).T
    return p


_PROG = None


def kernel(**inp):
    global _PROG
    inp = {k: np.asarray(v) for k, v in inp.items()}
    if _PROG is None:
        _PROG = Prog()
    prog = _PROG
    w_in = np.ascontiguousarray(inp["w_in"][:, :, KW0:KW1], dtype=np.float32)
    w_br = np.ascontiguousarray(np.stack([inp["w_br_a"], inp["w_br_b"], inp["w_br_c"], inp["w_br_d"]], axis=1))
    w_out = np.ascontiguousarray(inp["w_out"])
    w_pool = np.ascontiguousarray(inp["c_pool_w"])
    params = np.stack([_params(inp, l) for l in range(NL)], axis=0)
    gq = np.ascontiguousarray(np.broadcast_to(inp["q_norm_g"][:, None, :], (NL, 128, 128)))
    gk = np.ascontiguousarray(np.broadcast_to(inp["k_norm_g"][:, None, :], (NL, 128, 128)))
    in_maps = []
    for c in range(KCORES):
        b, half = c // 2, c % 2
        rc, rs = _rope_tables(half * T)
        m1, m1f, m16, ms = _masks(half == 1)
        invc = np.zeros((128, 4, 16), np.float32)
        for g, w in enumerate((2, 4, 8, 16)):
            pos = half * T + np.arange(16)
            invc[:, g, :] = (1.0 / np.minimum(pos + 1, w)).astype(np.float32)[None, :]
        in_maps.append({
            "xp": np.ascontiguousarray(inp["x_prompt"][b, half * T:(half + 1) * T]),
            "xs": np.ascontiguousarray(inp["x_sample"][c]),
            "ck": np.ascontiguousarray(inp["cache_attn_k"][:, c].reshape(NL, 2048, 1024)),
            "cv": np.ascontiguousarray(inp["cache_attn_v"][:, c].reshape(NL, 2048, 1024)),
            "sa": np.ascontiguousarray(inp["state_conv_a"][:, c]),
            "scs": np.ascontiguousarray(inp["state_pool_c"][:, c]),
            "sd": np.ascontiguousarray(inp["state_conv_d"][:, c]),
            "w_in": w_in, "w_br": w_br, "w_out": w_out, "w_pool": w_pool, "params": params, "gq": gq, "gk": gk,
            "ropec": rc, "ropes": rs, "m1": m1, "m1f": m1f, "m16": m16, "msk_s": ms,
            "flag": np.full((128, 1), float(half), np.float32), "invc": invc,
        })
    res = run_bass_kernel_spmd(prog.nc, in_maps, core_ids=list(range(KCORES)), **({'trace': True} if os.environ.get('KTRACE') else {}))
    global _RES
    _RES = res
    R = list(res.results)
    global _LAST
    _LAST = R[:KCORES]
    while len(R) < 8:
        R.append({k: np.zeros_like(np.asarray(v)) for k, v in R[0].items()})

    def g(c, name):
        return np.asarray(R[c][name])
    y_p = np.zeros((4, 2048, DM), np.float32)
    nk_p = np.zeros((NL, 4, 2048, 8, 128), np.float32)
    nv_p = np.zeros((NL, 4, 2048, 8, 128), np.float32)
    for c in range(8):
        b, half = c // 2, c % 2
        y_p[b, half * T:(half + 1) * T] = g(c, "yp")
        nk_p[:, b, half * T:(half + 1) * T] = g(c, "nkp").reshape(NL, T, 8, 128)
        nv_p[:, b, half * T:(half + 1) * T] = g(c, "nvp").reshape(NL, T, 8, 128)
    y_s = np.stack([g(c, "ys") for c in range(8)], axis=0)
    nca_p = np.stack([g(2 * b + 1, "ncap") for b in range(4)], axis=1)
    npc_p = np.stack([g(2 * b + 1, "npcp") for b in range(4)], axis=1)
    ncd_p = np.stack([g(2 * b + 1, "ncdp") for b in range(4)], axis=1)
    nk_s = np.stack([g(c, "nks").reshape(NL, 2048, 8, 128) for c in range(8)], axis=1)
    nv_s = np.stack([g(c, "nvs").reshape(NL, 2048, 8, 128) for c in range(8)], axis=1)
    nca_s = np.stack([g(c, "ncas") for c in range(8)], axis=1)
    npc_s = np.stack([g(c, "npcs") for c in range(8)], axis=1)
    ncd_s = np.stack([g(c, "ncds") for c in range(8)], axis=1)
    return (y_p, y_s, nk_p, nv_p, nca_p, npc_p, ncd_p, nk_s, nv_s, nca_s, npc_s, ncd_s)
```

```python
import os
import numpy as np
import ml_dtypes
import concourse.bass as bass
import concourse.mybir as mybir
from concourse.bass_utils import run_bass_kernel_spmd

F32 = mybir.dt.float32
BF16 = mybir.dt.bfloat16
AF = mybir.ActivationFunctionType
ALU = mybir.AluOpType
AX = mybir.AxisListType

DM = 2048
NIN = 23552
NL = 2
T = 1024
NT = 8
S = 4
TT = 1032
GW = 344
NKC = 16
(O_VA, O_CA, O_BA, O_ZA, O_Q, O_K, O_V, O_ZB, O_UC, O_ZC, O_GA, O_GB, O_ZD, O_GT) = (
    0, 1024, 2048, 3072, 4096, 7168, 8192, 9216, 10240, 11264, 12288, 13312, 14336, 15360)
HA, HC, HD = 2, 15, 30
EX_K, EX_V, EX_HA, EX_HC, EX_HD, R_EX = 0, 1024, 2048, 2050, 2065, 2096
EPS = 1e-6
PAST = 16384
P_NG, P_AW, P_CS, P_DW, P_DB, P_LG, P_LB, NPAR = 0, 16, 40, 48, 296, 304, 312, 320
NSLOT = 3
NDSEM = 40

STAGE = int(os.environ.get("KSTAGE", "99"))
KDBG = int(os.environ.get("KDBG", "0"))
KCORES = int(os.environ.get("KCORES", "8"))
KDUMP = int(os.environ.get("KDUMP", "0"))
KW0 = int(os.environ.get("KW0", "0"))
KW1 = int(os.environ.get("KW1", str(23552)))


class Res:
    __slots__ = ("name", "w", "r")

    def __init__(self, name):
        self.name = name
        self.w = None
        self.r = {}


class Sem:
    __slots__ = ("h", "idx", "val")

    def __init__(self, h, idx):
        self.h = h
        self.idx = idx
        self.val = 0


class Sched:
    def __init__(self, nc):
        self.nc = nc
        self.eng = {"pe": nc.tensor, "act": nc.scalar, "dve": nc.vector, "pool": nc.gpsimd, "sp": nc.sync}
        self.sem = {}
        n = 0
        for k in self.eng:
            self.sem[k] = Sem(nc.alloc_semaphore("sem_" + k), n)
            n += 1
        self.dsem = []
        for i in range(NDSEM):
            self.dsem.append(Sem(nc.alloc_semaphore("dsem%d" % i), n))
            n += 1
        self.ccsem = Sem(nc.alloc_semaphore("ccsem"), n)
        n += 1
        self.dnext = 0
        self.waited = {k: {} for k in self.eng}
        self.nwait = 0
        self.nop = 0

    def _wait(self, e, ev):
        sem, val = ev
        if e == "pe" and sem is self.sem["pe"]:
            return
        w = self.waited[e]
        if w.get(sem.idx, 0) < val:
            self.eng[e].wait_ge(sem.h, val)
            w[sem.idx] = val
            self.nwait += 1

    def deps(self, e, reads, writes):
        for r in reads:
            if r.w is not None:
                self._wait(e, r.w)
        for x in writes:
            if x.w is not None:
                self._wait(e, x.w)
            for idx, ev in x.r.items():
                self._wait(e, ev)

    def _record(self, ev, reads, writes):
        sem, val = ev
        for r in reads:
            old = r.r.get(sem.idx)
            if old is None or old[1] < val:
                r.r[sem.idx] = ev
        for x in writes:
            x.w = ev
            x.r = {}

    def op(self, e, fn, reads=(), writes=(), signal=True):
        self.deps(e, reads, writes)
        inst = fn(self.eng[e])
        sem = self.sem[e]
        if signal:
            sem.val += 1
            inst.then_inc(sem.h, 1)
            ev = (sem, sem.val)
        else:
            ev = (sem, sem.val + 1)
        self._record(ev, reads, writes)
        self.nop += 1
        return inst

    def dma(self, q, pairs, reads=(), writes=(), **kw):
        sem = self.dsem[self.dnext]
        self.dnext = (self.dnext + 1) % NDSEM
        if sem.val > 0:
            self._wait(q, (sem, sem.val))
        self.deps(q, reads, writes)
        for (o, i) in pairs:
            self.eng[q].dma_start(out=o, in_=i, **kw).then_inc(sem.h, 16)
            sem.val += 16
        ev = (sem, sem.val)
        self._record(ev, reads, writes)
        return ev

    def alias(self, new, old):
        for n_ in new:
            for o in old:
                if o.w is not None:
                    s, v = o.w
                    cur = n_.r.get(s.idx)
                    if cur is None or cur[1] < v:
                        n_.r[s.idx] = o.w
                for idx, ev in o.r.items():
                    cur = n_.r.get(idx)
                    if cur is None or cur[1] < ev[1]:
                        n_.r[idx] = ev


def mkap(base, dims):
    return bass.AP(base.tensor, base.offset, [list(base.ap[0])] + [list(d) for d in dims])


class Prog:
    def __init__(self):
        nc = bass.Bass("TRN2", target_bir_lowering=False)
        self.nc = nc
        self.sc = Sched(nc)
        self.declare_dram()
        self.alloc()
        self.setup()
        for l in range(NL if STAGE >= 99 else 1):
            self.layer(l)
        self.finish()

    def declare_dram(self):
        nc = self.nc

        def din(name, shape, dt=F32):
            return nc.dram_tensor(name, list(shape), dt, kind="ExternalInput").ap()

        def dout(name, shape, dt=F32):
            return nc.dram_tensor(name, list(shape), dt, kind="ExternalOutput").ap()

        self.xp = din("xp", [T, DM])
        self.xs = din("xs", [S, DM])
        self.ck = din("ck", [NL, 2048, 1024])
        self.cv = din("cv", [NL, 2048, 1024])
        self.sa = din("sa", [NL, HA, 1024])
        self.scs = din("scs", [NL, HC, 1024])
        self.sd = din("sd", [NL, HD, 1024])
        self.w_in_full = din("w_in", [NL, DM, KW1 - KW0])
        self.w_br = din("w_br", [NL, 4, 1024, DM])
        self.w_out = din("w_out", [NL, DM, DM])
        self.w_pool = din("w_pool", [NL, 4, 256, 256])
        self.params = din("params", [NL, 128, NPAR])
        self.gq = din("gq", [NL, 128, 128])
        self.gk = din("gk", [NL, 128, 128])
        self.ropec = din("ropec", [128, 9, 32])
        self.ropes = din("ropes", [128, 9, 32])
        self.m1 = din("m1", [128, 256], BF16)
        self.m1f = din("m1f", [128, 256], BF16)
        self.m16 = din("m16", [128, 64], BF16)
        self.msk_s = din("msk_s", [128, 17, 12], BF16)
        self.flag = din("flag", [128, 1])
        self.invc = din("invc", [128, 4, 16])

        self.yp = dout("yp", [T, DM])
        self.ys = dout("ys", [S, DM])
        self.nkp = dout("nkp", [NL, T, 1024])
        self.nvp = dout("nvp", [NL, T, 1024])
        self.ncap = dout("ncap", [NL, HA, 1024])
        self.npcp = dout("npcp", [NL, HC, 1024])
        self.ncdp = dout("ncdp", [NL, HD, 1024])
        self.nks = dout("nks", [NL, 2048, 1024])
        self.nvs = dout("nvs", [NL, 2048, 1024])
        self.ncas = dout("ncas", [NL, HA, 1024])
        self.npcs = dout("npcs", [NL, HC, 1024])
        self.ncds = dout("ncds", [NL, HD, 1024])

        self.dbg = dout("dbg", [4, 128, 8, 1062], BF16) if KDUMP else None
        self.x1 = nc.dram_tensor("x1", [T + S, DM], F32, kind="Internal").ap()
        self.exs_k = nc.dram_tensor("exs_k", [1024, 1024], BF16, kind="Internal").ap()
        self.exs_v = nc.dram_tensor("exs_v", [1024, 1024], BF16, kind="Internal").ap()
        self.exs_h = nc.dram_tensor("exs_h", [48, 1024], BF16, kind="Internal").ap()
        self.exd_k = nc.dram_tensor("exd_k", [2048, 1024], BF16, kind="Internal").ap()
        self.exd_v = nc.dram_tensor("exd_v", [2048, 1024], BF16, kind="Internal").ap()
        self.exd_h = nc.dram_tensor("exd_h", [96, 1024], BF16, kind="Internal").ap()
        self.r_exs_k, self.r_exs_v, self.r_exs_h = Res("exs_k"), Res("exs_v"), Res("exs_h")
        self.r_exd_k, self.r_exd_v, self.r_exd_h = Res("exd_k"), Res("exd_v"), Res("exd_h")
        self.r_x1 = Res("x1")
        self.out_res = Res("outs")

    def wi(self, l, col0, ncol):
        assert KW0 <= col0 and col0 + ncol <= KW1, (col0, ncol)
        return self.w_in_full[l, :, col0 - KW0:col0 - KW0 + ncol]

    def alloc(self):
        nc = self.nc
        self.sb_off = (nc.sbuf_base + 63) // 64 * 64
        self.sb_lim = nc.sbuf_top

        def A(name, shape, dt, off=None):
            nbytes = int(np.prod(shape[1:])) * (4 if dt == F32 else 2)
            nbytes = (nbytes + 31) // 32 * 32
            if off is None:
                off = self.sb_off
                self.sb_off += nbytes
                assert self.sb_off <= self.sb_lim, ("SBUF overflow", name, self.sb_off, self.sb_lim)
            return nc.alloc_sbuf_tensor_at(name, list(shape), dt, offset=off)
        self.A = A

        self.XNT = A("XNT", [128, NKC, TT], BF16)
        self.UA = A("UA", [128, 8, HA + TT], BF16)
        self.UC = A("UC", [128, 8, HC + TT + 1], BF16)
        self.UD = A("UD", [128, 8, HD + TT], BF16)
        self.YB = A("YB", [128, 8, TT], BF16)
        self.r_xnt = Res("xnt")
        self.r_ua = [Res("ua%d" % c) for c in range(8)]
        self.r_uc = [Res("uc%d" % c) for c in range(8)]
        self.r_ud = [Res("ud%d" % c) for c in range(8)]
        self.r_yb = [Res("yb%d" % c) for c in range(8)]
        self.WS = [A("WS%d" % i, [128, NKC, 256], BF16) for i in range(NSLOT)]
        self.r_ws = [Res("ws%d" % i) for i in range(NSLOT)]
        self.nslab = 0
        self.identb = A("identb", [128, 128], BF16)
        self.identf = A("identf", [128, 128], F32)
        self.onesb = A("onesb", [128, 128], BF16)
        self.onesm = A("onesm", [128, 128], BF16)
        self.M1 = A("M1", [128, 256], BF16)
        self.M1F = A("M1F", [128, 256], BF16)
        self.M16 = A("M16", [128, 64], BF16)
        self.MS = A("MS", [128, 17, 12], BF16)
        self.FLAG = A("FLAG", [128, 1], F32)
        self.EPSC = A("EPSC", [128, 1], F32)
        self.INVC = A("INVC", [128, 4, 16], F32)
        self.RC = A("RC", [128, 9, 32], F32)
        self.RS = A("RS", [128, 9, 32], F32)
        self.PAR = [A("PAR%d" % l, [128, NPAR], F32) for l in range(NL)]
        self.GQ = [A("GQ%d" % l, [128, 128], F32) for l in range(NL)]
        self.GK = [A("GK%d" % l, [128, 128], F32) for l in range(NL)]
        self.r_const = Res("const")
        self.UAL = A("UAL", [128, 8, 6], F32)
        self.UCL = A("UCL", [128, 8, 19], F32)
        self.UDL = A("UDL", [128, 8, 34], F32)
        self.ESA = A("ESA", [128, 8, HA + 4], F32)
        self.ESC = A("ESC", [128, 8, HC + 4], F32)
        self.ESD = A("ESD", [128, 8, HD + 4], F32)
        self.CSA = A("CSA", [128, 8, 4], F32)
        self.CSD = A("CSD", [128, 8, 4], F32)
        self.CST = A("CST", [128, 8, 4], F32)
        self.PL = [A("PL%d" % i, [128, 8, HC + 4], F32) for i in range(2)]
        self.PSS = A("PSS", [128, 8, 4], F32)
        self.ROWB = A("ROWB", [34, 1024], F32)
        self.STIN = A("STIN", [30, 1024], F32)
        self.KSB = A("KSB", [4, 1024], BF16)
        self.VSB = A("VSB", [4, 1024], BF16)
        self.HALO = A("HALO", [128, 8, 47], BF16)
        self.r_ual, self.r_ucl, self.r_udl = Res("ual"), Res("ucl"), Res("udl")
        self.r_esa, self.r_esc, self.r_esd = Res("esa"), Res("esc"), Res("esd")
        self.r_csa, self.r_csd, self.r_cst, self.r_pss = Res("csa"), Res("csd"), Res("cst"), Res("pss")
        self.r_pl = [Res("pl0"), Res("pl1")]
        self.r_rowb, self.r_stin = Res("rowb"), Res("stin")
        self.r_ksb, self.r_vsb, self.r_halo = Res("ksb"), Res("vsb"), Res("halo")
        self.SMALL = A("SMALL", [128, 64], F32)
        self.RTMP = A("RTMP", [128, 4, 32], F32)
        self.RTA = A("RTA", [128, 4, 32], F32)
        self.r_small = Res("small")

        xoff = self.sb_off
        self.MERGED = A("MERGED", [128, NKC, TT], BF16)
        self.r_merged = [Res("mg%d" % c) for c in range(NKC)]
        xend = self.sb_off
        self.sb_off = xoff
        self.KC = A("KC", [128, 16, 128], BF16)
        self.VC = A("VC", [128, 16, 128], BF16)
        self.KT = A("KT", [128, 16, 128], BF16)
        self.QT = A("QT", [128, 3, TT], BF16)
        self.QS = A("QS", [128, 12], BF16)
        self.NUM = A("NUM", [128, TT], F32)
        self.DEN = A("DEN", [128, TT], F32)
        self.PT = [A("PT%d" % i, [128, 256], BF16) for i in range(2)]
        self.PTS = A("PTS", [128, 17, 12], BF16)
        self.QF = [A("QF%d" % i, [128, 3, 128], F32) for i in range(2)]
        self.QSQ = A("QSQ", [128, 3, 128], F32)
        self.QB = [A("QB%d" % i, [128, 3, 128], BF16) for i in range(2)]
        self.KTN = A("KTN", [128, 4], BF16)
        self.QTN = [A("QTN%d" % i, [128, 384], BF16) for i in range(2)]
        self.r_qtn = [Res("qtn0"), Res("qtn1")]
        self.OSS = A("OSS", [128, 2, 12], F32)
        self.sb_off = max(xend, self.sb_off)
        self.r_kc, self.r_vc, self.r_kt = Res("kc"), Res("vc"), Res("kt")
        self.r_qt = [Res("qt%d" % g) for g in range(3)]
        self.r_qs, self.r_num, self.r_den = Res("qs"), Res("num"), Res("den")
        self.r_pt = [Res("pt0"), Res("pt1")]
        self.r_pts = Res("pts")
        self.r_qf = [Res("qf0"), Res("qf1")]
        self.r_qsq = Res("qsq")
        self.r_qb = [Res("qb0"), Res("qb1")]
        self.r_ktn, self.r_oss = Res("ktn"), Res("oss")
        self.attn_res = [self.r_kc, self.r_vc, self.r_kt] + self.r_qt + [self.r_qs, self.r_num, self.r_den] + \
            self.r_pt + [self.r_pts] + self.r_qf + [self.r_qsq] + self.r_qb + [self.r_ktn, self.r_oss] + self.r_qtn

        toff = self.sb_off
        TSIZE = 21 * 1024
        self.sb_off += TSIZE
        assert self.sb_off <= self.sb_lim, ("SBUF overflow T", self.sb_off, self.sb_lim)

        def TA(base):
            st = {"o": base}

            def f(name, shape, dt):
                t = A(name, shape, dt, off=st["o"])
                nb = (int(np.prod(shape[1:])) * (4 if dt == F32 else 2) + 31) // 32 * 32
                st["o"] += nb
                assert st["o"] <= toff + TSIZE, ("T region overflow", name)
                return t
            return f
        f = TA(toff)
        self.XST = [f("XST%d" % i, [128, DM], F32) for i in range(2)]
        self.XJ = f("XJ", [128, DM], BF16)
        self.r_xst = [Res("xst0"), Res("xst1")]
        self.r_xj = Res("xj")
        f = TA(toff)
        self.KSQ = f("KSQ", [128, 512], F32)
        self.KN = [f("KN%d" % i, [128, 512], F32) for i in range(2)]
        self.KB = [f("KB%d" % i, [128, 512], BF16) for i in range(2)]
        self.r_ksq = Res("ksq")
        self.r_kn = [Res("kn0"), Res("kn1")]
        self.r_kb = [Res("kb0"), Res("kb1")]
        self.r_rtmp, self.r_rta = Res("rtmp"), Res("rta")
        f = TA(toff)
        self.TF = [f("TF%d" % i, [128, TT + 16], F32) for i in range(4)]
        self.TB = [f("TB%d" % i, [128, TT], BF16) for i in range(2)]
        self.r_tf = [Res("tf%d" % i) for i in range(4)]
        self.r_tb = [Res("tb0"), Res("tb1")]
        f = TA(toff)
        self.XSL = [f("XSL%d" % i, [128, 512], F32) for i in range(2)]
        self.OSL = [f("OSL%d" % i, [128, 512], F32) for i in range(2)]
        self.r_xsl = [Res("xsl0"), Res("xsl1")]
        self.r_osl = [Res("osl0"), Res("osl1")]
        self.tres = {
            "p0": self.r_xst + [self.r_xj],
            "p1a": [self.r_ksq] + self.r_kn + self.r_kb,
            "p2": self.r_tf + self.r_tb,
            "p4": self.r_xsl + self.r_osl,
        }
        self.tcur = None
        self.sbuf_used = self.sb_off

        self.psA = nc.alloc_psum_tensor("psA", [128, 3, 512], F32)
        self.psB = nc.alloc_psum_tensor("psB", [128, 3, 512], F32)
        self.ps6 = nc.alloc_psum_tensor("ps6", [128, 512], F32)
        self.ps7 = nc.alloc_psum_tensor("ps7", [128, 512], F32)
        self.r_bank = [Res("bank%d" % i) for i in range(8)]

    def tphase(self, name):
        if self.tcur == name:
            return
        if self.tcur is not None:
            self.sc.alias(self.tres[name], self.tres[self.tcur])
        self.tcur = name

    def bank(self, b):
        if b < 3:
            return self.psA[:, b, :]
        if b < 6:
            return self.psB[:, b - 3, :]
        return (self.ps6 if b == 6 else self.ps7)[:, :]

    def unit(self, u):
        return (self.psA if u == 0 else self.psB), [self.r_bank[3 * u + j] for j in range(3)]

    def setup(self):
        nc, sc = self.nc, self.sc
        rc = self.r_const
        iot = self.SMALL
        self.tphase("p2")
        io = self.TF[0]
        sc.op("pool", lambda e: e.iota(io[:, 0:128], [[1, 128]], base=0, channel_multiplier=-1,
                                       allow_small_or_imprecise_dtypes=True), [], [self.r_tf[0]])
        sc.op("dve", lambda e: e.tensor_single_scalar(out=self.identb[:], in_=io[:, 0:128], scalar=0.0,
                                                      op=ALU.is_equal), [self.r_tf[0]], [rc])
        sc.op("dve", lambda e: e.tensor_single_scalar(out=self.identf[:], in_=io[:, 0:128], scalar=0.0,
                                                      op=ALU.is_equal), [self.r_tf[0]], [rc])
        sc.op("dve", lambda e: e.memset(self.onesb[:], 1.0), [], [rc])
        sc.op("dve", lambda e: e.memset(self.onesm[:], 1.0 / 1024.0), [], [rc])
        sc.op("dve", lambda e: e.memset(self.EPSC[:], EPS), [], [rc])
        sc.op("dve", lambda e: e.memset(self.XNT[:, :, 1028:1032], 0.0), [], [self.r_xnt])
        sc.op("dve", lambda e: e.memset(self.NUM[:, 1028:1032], 1.0), [], [self.r_num])
        sc.op("dve", lambda e: e.memset(self.DEN[:, 1028:1032], 1.0), [], [self.r_den])
        pairs = [(self.M1[:], self.m1), (self.M1F[:], self.m1f), (self.M16[:], self.m16), (self.MS[:], self.msk_s),
                 (self.FLAG[:], self.flag), (self.INVC[:], self.invc), (self.RC[:], self.ropec), (self.RS[:], self.ropes)]
        for l in range(NL):
            pairs += [(self.PAR[l][:], self.params[l]), (self.GQ[l][:], self.gq[l]), (self.GK[l][:], self.gk[l])]
        sc.dma("sp", pairs, [], [rc])
        for l in range(NL):
            for (dst, src) in ((self.nks, self.ck), (self.nvs, self.cv)):
                sc.dma("sp", [(dst[l, 0:2044, :], src[l, 4:2048, :])], [], [self.out_res])
            sc.dma("sp", [(self.npcs[l, 0:11, :], self.scs[l, 4:15, :]), (self.ncds[l, 0:26, :], self.sd[l, 4:30, :])],
                   [], [self.out_res])

    def load_slab(self, pieces, nk):
        slot = self.nslab % NSLOT
        self.nslab += 1
        ws = self.WS[slot]
        pairs = []
        for (src, c0) in pieces:
            ncol = src.shape[1]
            pairs.append((ws[:, 0:nk, c0:c0 + ncol], src.rearrange("(k p) n -> p k n", p=128)))
        self.sc.dma("pool", pairs, [], [self.r_ws[slot]])
        return slot

    def run_items(self, items):
        first = []
        n = 0
        for (loads, fn) in items:
            first.append(n)
            n += len(loads)
        flat = [ld for (loads, fn) in items for ld in loads]
        slots = [None] * len(flat)
        issued = 0
        for i, (loads, fn) in enumerate(items):
            a = first[i]
            b = a + len(loads)
            lim = min(len(flat), max(b, a + NSLOT))
            while issued < lim:
                slots[issued] = flat[issued]()
                issued += 1
            fn(slots[a:b])

    def fm_matmuls(self, u, slot, cb, nk, rhs_fn):
        ps, rb = self.unit(u)
        ws = self.WS[slot]
        for kc in range(nk):
            for g in range(3):
                last = (kc == nk - 1 and g == 2)
                rhs, rres = rhs_fn(kc, g)
                self.sc.op("pe", lambda e, kc=kc, g=g, rhs=rhs: e.matmul(
                    ps[:, g, 0:GW], lhsT=ws[:, kc, cb * 128:(cb + 1) * 128], rhs=rhs,
                    start=(kc == 0), stop=(kc == nk - 1)),
                    [self.r_ws[slot]] + rres, [rb[g]], signal=last)

    def xnt_rhs(self, kc, g):
        return self.XNT[:, kc, g * GW:(g + 1) * GW], [self.r_xnt]

    def uview(self, u):
        ps, rb = self.unit(u)
        return ps[:, :, 0:GW], rb

    @staticmethod
    def v3(ap2d):
        return ap2d.rearrange("p (g n) -> p g n", g=3)

    def tiles(self):
        for i in range(9):
            if i < 8:
                yield i, i * 128, 128
            else:
                yield i, 1024, 4

    def layer(self, l):
        steps = [self.phase0, self.phase1a, self.phase1b,
                 lambda l: self.collective(self.exs_h, self.exd_h, self.r_exs_h, self.r_exd_h),
                 self.phase2B, self.halos,
                 self.phase2A, self.phase2C, self.phase2D, self.dump, self.phase3, self.phase4]
        stage_of = [0, 1, 2, 3, 7, 3, 4, 5, 6, 0, 8, 9]
        kskip = int(os.environ.get("KSKIP", "0"))
        for idx, (fn, st) in enumerate(zip(steps, stage_of)):
            if kskip & (1 << idx):
                continue
            if fn == self.dump or STAGE >= st:
                fn(l)

    def dump(self, l):
        if KDUMP and l == 0:
            self.sc.dma("sp", [(self.dbg[0, :, :, 0:HA + TT], self.UA[:, :, :]), (self.dbg[1, :, :, 0:TT], self.YB[:, :, :]),
                               (self.dbg[2, :, :, 0:HC + TT], self.UC[:, :, 0:HC + TT]), (self.dbg[3, :, :, 0:HD + TT], self.UD[:, :, :])],
                        self.r_ua + self.r_yb + self.r_uc + self.r_ud, [self.out_res])

    def phase0(self, l):
        sc = self.sc
        self.tphase("p0")
        par = self.PAR[l]
        for i, tok0, rows in self.tiles():
            xb = self.XST[i % 2]
            rx = self.r_xst[i % 2]
            if l == 0:
                src = self.xp[tok0:tok0 + rows, :] if i < 8 else self.xs[:, :]
                rsrc = []
            else:
                src = self.x1[tok0:tok0 + rows, :]
                rsrc = [self.r_x1]
            sc.dma("sp", [(xb[0:rows, :], src)], rsrc, [rx])
            sm = self.SMALL
            sc.op("act", lambda e: e.activation(out=self.XJ[0:rows, :], in_=xb[0:rows, :], func=AF.Square,
                                                accum_out=sm[0:rows, 0:1]), [rx], [self.r_xj, self.r_small])
            sc.op("dve", lambda e: e.tensor_scalar(out=sm[0:rows, 1:2], in0=sm[0:rows, 0:1], scalar1=1.0 / DM,
                                                   scalar2=EPS, op0=ALU.mult, op1=ALU.add),
                  [self.r_small], [self.r_small])
            sc.op("act", lambda e: e.activation(out=sm[0:rows, 2:3], in_=sm[0:rows, 1:2], func=AF.Sqrt),
                  [self.r_small], [self.r_small])
            sc.op("dve", lambda e: e.reciprocal(out=sm[0:rows, 3:4], in_=sm[0:rows, 2:3]),
                  [self.r_small], [self.r_small])
            sc.op("dve", lambda e: e.tensor_scalar(out=xb[0:rows, :], in0=xb[0:rows, :], scalar1=sm[0:rows, 3:4],
                                                   scalar2=None, op0=ALU.mult), [self.r_small, rx], [rx])
            for q in range(4):
                b = 6 + (q % 2)
                pb = self.bank(b)
                for j in range(4):
                    kc = q * 4 + j
                    sc.op("pe", lambda e, kc=kc, j=j, pb=pb: e.transpose(
                        out=pb[:, j * 128:j * 128 + rows], in_=xb[0:rows, kc * 128:(kc + 1) * 128],
                        identity=self.identf[0:rows, 0:rows]), [rx, self.r_const], [self.r_bank[b]])
                for j in range(4):
                    kc = q * 4 + j
                    eng = "act" if j % 2 == 0 else "dve"
                    if eng == "act":
                        sc.op("act", lambda e, kc=kc, j=j, pb=pb: e.activation(
                            out=self.XNT[:, kc, tok0:tok0 + rows], in_=pb[:, j * 128:j * 128 + rows],
                            func=AF.Copy, scale=par[:, P_NG + kc:P_NG + kc + 1]),
                            [self.r_bank[b], self.r_const], [self.r_xnt])
                    else:
                        sc.op("dve", lambda e, kc=kc, j=j, pb=pb: e.tensor_scalar(
                            out=self.XNT[:, kc, tok0:tok0 + rows], in0=pb[:, j * 128:j * 128 + rows],
                            scalar1=par[:, P_NG + kc:P_NG + kc + 1], scalar2=None, op0=ALU.mult),
                            [self.r_bank[b], self.r_const], [self.r_xnt])

    def norm_rope(self, src_ps, rows, nh, gain, i, dst, r_dst, rbank):
        sc = self.sc
        sm = self.SMALL
        sq = self.KSQ if nh != 3 else self.QSQ
        rsq = self.r_ksq if nh != 3 else self.r_qsq
        src3 = src_ps.rearrange("p (h c) -> p h c", h=nh)
        sq3 = (sq[0:rows, 0:nh * 128].rearrange("p (h c) -> p h c", h=nh)) if nh != 3 else sq[0:rows, :, :]
        sc.op("act", lambda e: e.activation(out=sq3, in_=src3, func=AF.Square), rbank, [rsq])
        sc.op("dve", lambda e: e.tensor_reduce(out=sm[0:rows, 8:8 + nh], in_=sq3, axis=AX.X, op=ALU.add),
              [rsq], [self.r_small])
        sc.op("act", lambda e: e.activation(out=sm[0:rows, 16:16 + nh], in_=sm[0:rows, 8:8 + nh], func=AF.Sqrt,
                                            bias=self.EPSC[0:rows, 0:1], scale=1.0 / 128), [self.r_small, self.r_const], [self.r_small])
        sc.op("dve", lambda e: e.reciprocal(out=sm[0:rows, 24:24 + nh], in_=sm[0:rows, 16:16 + nh]),
              [self.r_small], [self.r_small])
        for h in range(nh):
            sc.op("dve", lambda e, h=h: e.scalar_tensor_tensor(out=dst[:, h, :], in0=src3[:, h, :], scalar=sm[0:rows, 24 + h:25 + h],
                                                             in1=gain[0:rows, :], op0=ALU.mult, op1=ALU.mult),
                  rbank + [self.r_small, self.r_const], [r_dst])
        ta = self.RTA[0:rows, 0:nh, :]
        tb = self.RTMP[0:rows, 0:nh, :]
        cc = mkap(self.RC[0:rows, i, 0:1], [[0, nh], [1, 32]])
        s_lo = mkap(self.RS[0:rows, i, 0:1], [[0, nh], [1, 16]])
        s_hi = mkap(self.RS[0:rows, i, 16:17], [[0, nh], [1, 16]])
        sc.op("dve", lambda e: e.tensor_tensor(out=ta, in0=dst[:, :, 0:32], in1=cc, op=ALU.mult),
              [r_dst, self.r_const], [self.r_rta])
        sc.op("dve", lambda e: e.tensor_tensor(out=tb[:, :, 0:16], in0=dst[:, :, 16:32], in1=s_lo, op=ALU.mult),
              [r_dst, self.r_const], [self.r_rtmp])
        sc.op("dve", lambda e: e.tensor_tensor(out=tb[:, :, 16:32], in0=dst[:, :, 0:16], in1=s_hi, op=ALU.mult),
              [r_dst, self.r_const], [self.r_rtmp])
        sc.op("dve", lambda e: e.tensor_tensor(out=dst[:, :, 0:32], in0=ta, in1=tb, op=ALU.add),
              [self.r_rta, self.r_rtmp], [r_dst])

    def phase1a(self, l):
        sc = self.sc
        self.tphase("p1a")
        items = []
        cnt = {"n": 0}
        for s in range(4):
            colb = (O_K if s < 2 else O_V) + (s % 2) * 512
            lda = (lambda colb=colb: self.load_slab([(self.wi(l, colb, 256), 0)], NKC))
            ldb = (lambda colb=colb: self.load_slab([(self.wi(l, colb + 256, 256), 0)], NKC))

            def fn(slots, s=s):
                isk = s < 2
                c0 = (s % 2) * 512
                for i, tok0, rows in self.tiles():
                    n = cnt["n"]
                    cnt["n"] += 1
                    b = n % 4
                    pb = self.bank(b)
                    rb = [self.r_bank[b]]
                    for hf, slot in enumerate(slots):
                        ws = self.WS[slot]
                        for kc in range(NKC):
                            sc.op("pe", lambda e, kc=kc, ws=ws, hf=hf: e.matmul(
                                pb[0:rows, hf * 256:(hf + 1) * 256], lhsT=self.XNT[:, kc, tok0:tok0 + rows], rhs=ws[:, kc, 0:256],
                                start=(kc == 0), stop=(kc == NKC - 1)),
                                [self.r_xnt, self.r_ws[slot]], rb, signal=(kc == NKC - 1 and hf == 1))
                    kn = self.KN[n % 2]
                    rkn = self.r_kn[n % 2]
                    kb = self.KB[n % 2]
                    rkb = self.r_kb[n % 2]
                    if isk:
                        dst = kn[0:rows, :].rearrange("p (h c) -> p h c", h=4)
                        self.norm_rope(pb[0:rows, 0:512], rows, 4, self.GK[l], i, dst, rkn, rb)
                        outd, ext_, rext = (self.nkp, self.exs_k, self.r_exs_k)
                    else:
                        sc.op("act", lambda e: e.activation(out=kn[0:rows, :], in_=pb[0:rows, 0:512], func=AF.Copy),
                              rb, [rkn])
                        outd, ext_, rext = (self.nvp, self.exs_v, self.r_exs_v)
                    sc.op("act", lambda e: e.activation(out=kb[0:rows, :], in_=kn[0:rows, :], func=AF.Copy),
                          [rkn], [rkb])
                    if i < 8:
                        sc.dma("sp", [(outd[l, tok0:tok0 + rows, c0:c0 + 512], kn[0:rows, :])], [rkn], [self.out_res])
                        sc.dma("sp", [(ext_[tok0:tok0 + rows, c0:c0 + 512], kb[0:rows, :])], [rkb], [rext])
                    else:
                        outs = self.nks if isk else self.nvs
                        sc.dma("sp", [(outs[l, 2044:2048, c0:c0 + 512], kn[0:rows, :])], [rkn], [self.out_res])
                        sbt, rsb = (self.KSB, self.r_ksb) if isk else (self.VSB, self.r_vsb)
                        sc.op("act", lambda e: e.activation(out=sbt[0:4, c0:c0 + 512], in_=kn[0:rows, :],
                                                            func=AF.Copy), [rkn], [rsb])
            items.append(([lda, ldb], fn))
        self.run_items(items)
        self.collective(self.exs_k, self.exd_k, self.r_exs_k, self.r_exd_k)
        self.collective(self.exs_v, self.exd_v, self.r_exs_v, self.r_exd_v)

    def ex_view(self, ap, row0, c, t):
        return bass.AP(ap.tensor, row0 * 1024, [[c * t, 128], [t, c], [1, t]])

    def state_out(self, src, n, dsts):
        sc = self.sc
        for hb in range(2):
            b = 6 + hb
            pb = self.bank(b)
            for j in range(4):
                c = hb * 4 + j
                sc.op("pe", lambda e, c=c, j=j: e.transpose(out=pb[0:n, j * 128:(j + 1) * 128], in_=src[:, c, 0:n],
                                                          identity=self.identf[:, :]),
                      [self.r_const, self.r_ual, self.r_ucl, self.r_udl], [self.r_bank[b]])
            sc.op("act", lambda e: e.activation(out=self.ROWB[0:n, hb * 512:(hb + 1) * 512], in_=pb[0:n, :],
                                                func=AF.Copy), [self.r_bank[b]], [self.r_rowb])
        pairs = [(d, self.ROWB[r0:r1, :]) for (d, r0, r1) in dsts]
        sc.dma("sp", pairs, [self.r_rowb], [self.out_res])

    def state_in(self, src_dram, n, dst):
        sc = self.sc
        sc.dma("sp", [(self.STIN[0:n, :], src_dram)], [], [self.r_stin])
        pb = self.bank(7)
        for c in range(8):
            sc.op("pe", lambda e, c=c: e.transpose(out=pb[:, c * 32:c * 32 + n], in_=self.STIN[0:n, c * 128:(c + 1) * 128],
                                                    identity=self.identf[0:n, 0:n]),
                  [self.r_const, self.r_stin], [self.r_bank[7]])
        src = pb[:, 0:256].rearrange("p (c t) -> p c t", c=8)[:, :, 0:n]
        sc.op("act", lambda e: e.activation(out=dst[:, :, 0:n], in_=src, func=AF.Copy),
              [self.r_bank[7]], [self.r_esa, self.r_esc, self.r_esd])

    def phase1b(self, l):
        sc = self.sc
        self.tphase("p2")
        par = self.PAR[l]
        self.state_in(self.sa[l], HA, self.ESA)
        self.state_in(self.scs[l], HC, self.ESC)
        self.state_in(self.sd[l], HD, self.ESD)
        items = []
        for c in range(8):
            ld = (lambda c=c: self.load_slab([(self.wi(l, O_VA + c * 128, 128), 0), (self.wi(l, O_CA + c * 128, 128), 128)], NKC))

            def fnA(slots, c=c):
                slot = slots[0]
                self.fm_matmuls(0, slot, 0, NKC, self.xnt_rhs)
                self.fm_matmuls(1, slot, 1, NKC, self.xnt_rhs)
                u0, rb0 = self.uview(0)
                u1, rb1 = self.uview(1)
                tf, rtf = self.TF[c % 2], self.r_tf[c % 2]
                sc.op("act", lambda e: e.activation(out=self.v3(tf[:, 0:TT]), in_=u0, func=AF.Copy), rb0, [rtf])
                sc.op("dve", lambda e: e.tensor_tensor(out=self.v3(self.UA[:, c, HA:HA + TT]), in0=u1,
                                                       in1=self.v3(tf[:, 0:TT]), op=ALU.mult), rb1 + [rtf], [self.r_ua[c]])
                sc.op("dve", lambda e: e.tensor_tensor(out=self.UAL[:, c, 0:6], in0=self.psB[:, 2, 334:340],
                                                       in1=tf[:, 1022:1028], op=ALU.mult), rb1 + [rtf], [self.r_ual])
            items.append(([ld], fnA))
        for cp in range(4):
            ld = (lambda cp=cp: self.load_slab([(self.wi(l, O_UC + cp * 256, 256), 0)], NKC))

            def fnC(slots, cp=cp):
                slot = slots[0]
                for j in range(2):
                    c = cp * 2 + j
                    self.fm_matmuls(j, slot, j, NKC, self.xnt_rhs)
                    u, rb = self.uview(j)
                    ps, _ = self.unit(j)
                    sc.op("act", lambda e: e.activation(out=self.v3(self.UC[:, c, HC:HC + TT]), in_=u, func=AF.Copy),
                          rb, [self.r_uc[c]])
                    sc.op("dve", lambda e: e.tensor_copy(out=self.UCL[:, c, 0:19], in_=ps[:, 2, 321:340]),
                          rb, [self.r_ucl])
            items.append(([ld], fnC))
        for c in range(8):
            ld = (lambda c=c: self.load_slab([(self.wi(l, O_GA + c * 128, 128), 0), (self.wi(l, O_GB + c * 128, 128), 128)], NKC))

            def fnD(slots, c=c):
                slot = slots[0]
                self.fm_matmuls(0, slot, 0, NKC, self.xnt_rhs)
                self.fm_matmuls(1, slot, 1, NKC, self.xnt_rhs)
                u0, rb0 = self.uview(0)
                u1, rb1 = self.uview(1)
                tf, rtf = self.TF[c % 2], self.r_tf[c % 2]
                sc.op("act", lambda e: e.activation(out=self.v3(tf[:, 0:TT]), in_=u1, func=AF.Sigmoid), rb1, [rtf])
                sc.op("dve", lambda e: e.tensor_tensor(out=self.v3(self.UD[:, c, HD:HD + TT]), in0=u0,
                                                       in1=self.v3(tf[:, 0:TT]), op=ALU.mult), rb0 + [rtf], [self.r_ud[c]])
                sc.op("dve", lambda e: e.tensor_tensor(out=self.UDL[:, c, 0:34], in0=self.psA[:, 2, 306:340],
                                                       in1=tf[:, 994:1028], op=ALU.mult), rb0 + [rtf], [self.r_udl])
            items.append(([ld], fnD))
        self.run_items(items)
        sc.dma("sp", [(self.ex_view(self.exs_h, 0, 8, HA), self.UA[:, :, HA + 1022:HA + 1024]),
                      (self.ex_view(self.exs_h, 2, 8, HC), self.UC[:, :, HC + 1009:HC + 1024]),
                      (self.ex_view(self.exs_h, 17, 8, HD), self.UD[:, :, HD + 994:HD + 1024])],
               self.r_ua + self.r_uc + self.r_ud, [self.r_exs_h])
        self.state_out(self.UAL, 6, [(self.ncap[l], 0, 2), (self.ncas[l], 4, 6)])
        self.state_out(self.UCL, 19, [(self.npcp[l], 0, 15), (self.npcs[l, 11:15, :], 15, 19)])
        self.state_out(self.UDL, 34, [(self.ncdp[l], 0, 30), (self.ncds[l, 26:30, :], 30, 34)])
        sc.op("pool", lambda e: e.tensor_copy(out=self.ESA[:, :, HA:HA + 4], in_=self.UAL[:, :, 2:6]), [self.r_ual], [self.r_esa])
        sc.op("pool", lambda e: e.tensor_copy(out=self.ESC[:, :, HC:HC + 4], in_=self.UCL[:, :, 15:19]), [self.r_ucl], [self.r_esc])
        sc.op("pool", lambda e: e.tensor_copy(out=self.ESD[:, :, HD:HD + 4], in_=self.UDL[:, :, 30:34]), [self.r_udl], [self.r_esd])

        def wb(col):
            return mkap(par[:, col:col + 1], [[1, 8], [0, 4]])
        for (ES, res_es, CS, res_cs, ntap, pcol, bias) in ((self.ESA, self.r_esa, self.CSA, self.r_csa, 3, P_AW, None),
                                                          (self.ESD, self.r_esd, self.CSD, self.r_csd, 31, P_DW, P_DB)):
            for j in range(ntap):
                if j == 0:
                    sc.op("pool", lambda e: e.tensor_tensor(out=CS[:, :, :], in0=ES[:, :, 0:4], in1=wb(pcol), op=ALU.mult),
                          [res_es, self.r_const], [res_cs])
                else:
                    sc.op("pool", lambda e, j=j: e.tensor_tensor(out=self.CST[:, :, :], in0=ES[:, :, j:j + 4], in1=wb(pcol + j * 8),
                                                              op=ALU.mult), [res_es, self.r_const], [self.r_cst])
                    sc.op("pool", lambda e: e.tensor_tensor(out=CS[:, :, :], in0=CS[:, :, :], in1=self.CST[:, :, :], op=ALU.add),
                          [self.r_cst], [res_cs])
            if bias is not None:
                sc.op("pool", lambda e: e.tensor_tensor(out=CS[:, :, :], in0=CS[:, :, :], in1=wb(bias), op=ALU.add),
                      [self.r_const], [res_cs])
        E = self.ESC
        L = self.PL
        n = HC + 4
        steps = ((1, E, L[0]), (2, L[0], L[1]), (4, L[1], L[0]), (8, L[0], L[1]))
        for g, (sh, src, dst) in enumerate(steps):
            lo = 2 * sh - 1
            rsrc = self.r_esc if src is E else self.r_pl[0 if src is L[0] else 1]
            rdst = self.r_pl[0 if dst is L[0] else 1]
            sc.op("dve", lambda e: e.tensor_tensor(out=dst[:, :, lo:n], in0=src[:, :, lo:n], in1=src[:, :, lo - sh:n - sh], op=ALU.add),
                  [rsrc], [rdst])
            w = 2 * sh
            sc.op("dve", lambda e: e.scalar_tensor_tensor(out=self.PSS[:, 2 * g:2 * g + 2, :], in0=dst[:, 2 * g:2 * g + 2, HC:HC + 4],
                                                          scalar=1.0 / w, in1=E[:, 2 * g:2 * g + 2, HC:HC + 4],
                                                          op0=ALU.mult, op1=ALU.subtract), [rdst, self.r_esc], [self.r_pss])

    def collective(self, src, dst, rsrc, rdst):
        sc = self.sc
        sc.deps("pool", [rsrc], [rdst])
        groups = [[2 * i, 2 * i + 1] for i in range(KCORES // 2)]
        cs = sc.ccsem
        self.nc.gpsimd.collective_compute("AllGather", ALU.bypass, replica_groups=groups,
                                          ins=[src], outs=[dst]).then_inc(cs.h, 1)
        cs.val += 1
        sc._record((cs, cs.val), [rsrc], [rdst])

    def halos(self, l):
        sc = self.sc
        for (U, ru, row0, H, o) in ((self.UA, self.r_ua, 0, HA, 0), (self.UC, self.r_uc, 2, HC, 2),
                                    (self.UD, self.r_ud, 17, HD, 17)):
            sc.dma("sp", [(self.HALO[:, :, o:o + H], self.ex_view(self.exd_h, row0, 8, H))], [self.r_exd_h], [self.r_halo])
            sc.op("dve", lambda e: e.tensor_scalar(out=U[:, :, 0:H], in0=self.HALO[:, :, o:o + H], scalar1=self.FLAG[:, 0:1],
                                                   scalar2=None, op0=ALU.mult), [self.r_halo, self.r_const], ru)

    def phase2A(self, l):
        sc = self.sc
        self.tphase("p2")
        par = self.PAR[l]
        items = []
        for c in range(8):
            ld = (lambda c=c: self.load_slab([(self.wi(l, O_BA + c * 128, 128), 0), (self.wi(l, O_ZA + c * 128, 128), 128)], NKC))

            def fn(slots, c=c):
                slot = slots[0]
                self.fm_matmuls(0, slot, 0, NKC, self.xnt_rhs)
                self.fm_matmuls(1, slot, 1, NKC, self.xnt_rhs)
                u0, rb0 = self.uview(0)
                u1, rb1 = self.uview(1)
                t3, rt3 = self.TF[(c % 2) * 2], self.r_tf[(c % 2) * 2]
                sz, rsz = self.TF[(c % 2) * 2 + 1], self.r_tf[(c % 2) * 2 + 1]
                ua = self.UA
                wc = [par[:, P_AW + j * 8 + c:P_AW + j * 8 + c + 1] for j in range(3)]
                sc.op("dve", lambda e: e.tensor_scalar(out=t3[:, 0:TT], in0=ua[:, c, 0:TT], scalar1=wc[0], scalar2=None,
                                                       op0=ALU.mult), [self.r_ua[c], self.r_const], [rt3])
                for j in (1, 2):
                    sc.op("dve", lambda e, j=j: e.scalar_tensor_tensor(out=t3[:, 0:TT], in0=ua[:, c, j:j + TT], scalar=wc[j],
                                                                     in1=t3[:, 0:TT], op0=ALU.mult, op1=ALU.add),
                          [self.r_ua[c], self.r_const], [rt3])
                sc.op("pool", lambda e: e.tensor_copy(out=t3[:, 1024:1028], in_=self.CSA[:, c, :]), [self.r_csa], [rt3])
                sc.op("act", lambda e: e.activation(out=self.v3(sz[:, 0:TT]), in_=u1, func=AF.Silu), rb1, [rsz])
                sc.op("dve", lambda e: e.tensor_tensor(out=self.v3(t3[:, 0:TT]), in0=u0, in1=self.v3(t3[:, 0:TT]), op=ALU.mult),
                      rb0, [rt3])
                sc.op("dve", lambda e: e.tensor_tensor(out=ua[:, c, HA:HA + TT], in0=t3[:, 0:TT], in1=sz[:, 0:TT], op=ALU.mult),
                      [rt3, rsz], [self.r_ua[c]])
            items.append(([ld], fn))
        self.run_items(items)

    def phase2C(self, l):
        sc = self.sc
        self.tphase("p2")
        par = self.PAR[l]
        items = []
        for g in range(4):
            ldp = (lambda g=g: self.load_slab([(self.w_pool[l, g], 0)], 2))
            ldz = (lambda g=g: self.load_slab([(self.wi(l, O_ZC + g * 256, 256), 0)], NKC))

            def fn(slots, g=g):
                sp_, sz_ = slots
                w = 2 ** (g + 1)
                n = HC + TT
                for j in range(2):
                    c = 2 * g + j
                    ext = self.UC[:, c, :]
                    la, lb = self.TF[0], self.TF[1]
                    ra, rb_ = self.r_tf[0], self.r_tf[1]
                    src, rsrc = ext, self.r_uc[c]
                    dst, rdst = la, ra
                    sh = 1
                    for lev in range(g + 1):
                        lo = 2 * sh - 1
                        sc.op("dve", lambda e, src=src, dst=dst, lo=lo, sh=sh: e.tensor_tensor(
                            out=dst[:, lo:n], in0=src[:, lo:n], in1=src[:, lo - sh:n - sh], op=ALU.add), [rsrc], [rdst])
                        src, rsrc = dst, rdst
                        dst, rdst = (lb, rb_) if dst is la else (la, ra)
                        sh *= 2
                    lf, rlf = src, rsrc
                    pt, rpt = self.TB[j], self.r_tb[j]
                    sc.op("dve", lambda e: e.scalar_tensor_tensor(out=pt[:, 0:TT], in0=lf[:, HC:HC + TT], scalar=1.0 / w,
                                                                  in1=ext[:, HC:HC + TT], op0=ALU.mult, op1=ALU.subtract),
                          [rlf, self.r_uc[c]], [rpt])
                    tmp = self.SMALL[:, 32:48]
                    sc.op("dve", lambda e: e.tensor_tensor(out=tmp, in0=lf[:, HC:HC + 16], in1=self.INVC[:, g, :], op=ALU.mult),
                          [rlf, self.r_const], [self.r_small])
                    sc.op("dve", lambda e: e.tensor_tensor(out=pt[:, 0:16], in0=tmp, in1=ext[:, HC:HC + 16], op=ALU.subtract),
                          [self.r_small, self.r_uc[c]], [rpt])
                    sc.op("pool", lambda e: e.tensor_copy(out=pt[:, 1024:1028], in_=self.PSS[:, c, :]), [self.r_pss], [rpt])
                for j in range(2):
                    c = 2 * g + j
                    self.fm_matmuls(0, sp_, j, 2, lambda kc, gi: (self.TB[kc][:, gi * GW:(gi + 1) * GW], [self.r_tb[kc]]))
                    self.fm_matmuls(1, sz_, j, NKC, self.xnt_rhs)
                    u0, rb0 = self.uview(0)
                    u1, rb1 = self.uview(1)
                    sz, rsz = self.TF[2 + j], self.r_tf[2 + j]
                    sc.op("act", lambda e: e.activation(out=self.v3(sz[:, 0:TT]), in_=u1, func=AF.Silu), rb1, [rsz])
                    sc.op("dve", lambda e: e.scalar_tensor_tensor(out=self.v3(self.UC[:, c, HC:HC + TT]), in0=u0,
                                                                  scalar=par[:, P_CS + c:P_CS + c + 1], in1=self.v3(sz[:, 0:TT]),
                                                                  op0=ALU.mult, op1=ALU.mult),
                          rb0 + [rsz, self.r_const], [self.r_uc[c]])
            items.append(([ldp, ldz], fn))
        self.run_items(items)

    def phase2D(self, l):
        sc = self.sc
        self.tphase("p2")
        par = self.PAR[l]
        items = []
        cntu = {"n": 0}
        for c in range(8):
            def ldg(c=c):
                slot = self.nslab % NSLOT
                self.nslab += 1
                ws = self.WS[slot]
                sc.deps("pool", [], [self.r_ws[slot]])
                for j in range(31):
                    dst_ = ws[:, j // 2, (j % 2) * 128:(j % 2) * 128 + 128]
                    wcol = par[:, P_DW + j * 8 + c:P_DW + j * 8 + c + 1]
                    if j % 2 == 0:
                        sc.op("dve", lambda e, dst_=dst_, wcol=wcol: e.tensor_scalar(out=dst_, in0=self.identb[:, :], scalar1=wcol, scalar2=None,
                                                                                   op0=ALU.mult), [self.r_const], [self.r_ws[slot]])
                    else:
                        sc.op("act", lambda e, dst_=dst_, wcol=wcol: e.activation(out=dst_, in_=self.identb[:, :], func=AF.Copy, scale=wcol),
                              [self.r_const], [self.r_ws[slot]])
                return slot

            def fn(slots, c=c):
                slot = slots[0]
                ws = self.WS[slot]
                u = cntu["n"] % 2
                cntu["n"] += 1
                ps, rb = self.unit(u)
                for j in range(31):
                    for gi in range(3):
                        sc.op("pe", lambda e, j=j, gi=gi: e.matmul(
                            ps[:, gi, 0:GW], lhsT=ws[:, j // 2, (j % 2) * 128:(j % 2) * 128 + 128],
                            rhs=self.UD[:, c, j + gi * GW:j + gi * GW + GW], start=(j == 0), stop=(j == 30)),
                            [self.r_ws[slot], self.r_ud[c]], [rb[gi]], signal=(j == 30 and gi == 2))
                sc.op("act", lambda e: e.activation(out=self.v3(self.UD[:, c, HD:HD + TT]), in_=ps[:, :, 0:GW], func=AF.Identity,
                                                    bias=par[:, P_DB + c:P_DB + c + 1], scale=1.0), rb + [self.r_const], [self.r_ud[c]])
                sc.op("pool", lambda e: e.tensor_copy(out=self.UD[:, c, HD + 1024:HD + 1028], in_=self.CSD[:, c, :]),
                      [self.r_csd], [self.r_ud[c]])
            items.append(([ldg], fn))
        self.run_items(items)
        pm, rbm = self.unit(0)
        pe2, rbe = self.unit(1)
        for c in range(8):
            sq, rsq = self.TB[c % 2], self.r_tb[c % 2]
            sc.op("act", lambda e: e.activation(out=sq[:, 0:TT], in_=self.UD[:, c, HD:HD + TT], func=AF.Square),
                  [self.r_ud[c]], [rsq])
            for gi in range(3):
                sc.op("pe", lambda e, gi=gi: e.matmul(pm[:, gi, 0:GW], lhsT=self.onesm[:, :],
                                                    rhs=self.UD[:, c, HD + gi * GW:HD + (gi + 1) * GW], start=(c == 0), stop=(c == 7)),
                      [self.r_const, self.r_ud[c]], [rbm[gi]], signal=True)
                sc.op("pe", lambda e, gi=gi: e.matmul(pe2[:, gi, 0:GW], lhsT=self.onesm[:, :],
                                                    rhs=sq[:, gi * GW:(gi + 1) * GW], start=(c == 0), stop=(c == 7)),
                      [self.r_const, rsq], [rbe[gi]], signal=True)
        mean, rmean = self.TF[0], self.r_tf[0]
        rstd, rrstd = self.TF[1], self.r_tf[1]
        sc.op("act", lambda e: e.activation(out=self.v3(mean[:, 0:TT]), in_=pm[:, :, 0:GW], func=AF.Copy), rbm, [rmean])
        sc.op("dve", lambda e: e.tensor_tensor(out=rstd[:, 0:TT], in0=mean[:, 0:TT], in1=mean[:, 0:TT], op=ALU.mult), [rmean], [rrstd])
        sc.op("dve", lambda e: e.tensor_tensor(out=self.v3(rstd[:, 0:TT]), in0=pe2[:, :, 0:GW], in1=self.v3(rstd[:, 0:TT]),
                                               op=ALU.subtract), rbe, [rrstd])
        sc.op("dve", lambda e: e.tensor_scalar(out=rstd[:, 0:TT], in0=rstd[:, 0:TT], scalar1=0.0, scalar2=EPS, op0=ALU.max, op1=ALU.add),
              [], [rrstd])
        sc.op("act", lambda e: e.activation(out=rstd[:, 0:TT], in_=rstd[:, 0:TT], func=AF.Sqrt), [], [rrstd])
        sc.op("dve", lambda e: e.reciprocal(out=rstd[:, 0:TT], in_=rstd[:, 0:TT]), [], [rrstd])
        items = []
        for cp in range(4):
            ld = (lambda cp=cp: self.load_slab([(self.wi(l, O_ZD + cp * 256, 256), 0)], NKC))

            def fn2(slots, cp=cp):
                slot = slots[0]
                for j in range(2):
                    c = 2 * cp + j
                    self.fm_matmuls(j, slot, j, NKC, self.xnt_rhs)
                    u, rb = self.uview(j)
                    sz, rsz = self.TF[2], self.r_tf[2]
                    t, rt = self.TF[3], self.r_tf[3]
                    ud = self.UD[:, c, HD:HD + TT]
                    sc.op("dve", lambda e: e.tensor_tensor(out=t[:, 0:TT], in0=ud, in1=mean[:, 0:TT], op=ALU.subtract),
                          [self.r_ud[c], rmean], [rt])
                    sc.op("dve", lambda e: e.tensor_tensor(out=t[:, 0:TT], in0=t[:, 0:TT], in1=rstd[:, 0:TT], op=ALU.mult),
                          [rrstd], [rt])
                    sc.op("act", lambda e: e.activation(out=t[:, 0:TT], in_=t[:, 0:TT], func=AF.Silu,
                                                        bias=par[:, P_LB + c:P_LB + c + 1], scale=par[:, P_LG + c:P_LG + c + 1]),
                          [self.r_const], [rt])
                    sc.op("act", lambda e: e.activation(out=self.v3(sz[:, 0:TT]), in_=u, func=AF.Silu), rb, [rsz])
                    sc.op("dve", lambda e: e.tensor_tensor(out=ud, in0=t[:, 0:TT], in1=sz[:, 0:TT], op=ALU.mult),
                          [rt, rsz], [self.r_ud[c]])
            items.append(([ld], fn2))
        self.run_items(items)

    def phase2B(self, l):
        sc = self.sc
        self.tphase("p2")
        self.sc.alias(self.attn_res, self.r_merged)
        nc = self.nc
        SCALE = 128.0 ** -0.5
        psT2 = self.psA[:].bitcast(BF16)[:, 2, :]
        rb2 = [self.r_bank[2]]
        items = []

        def kv_ap(t, row0, h, dims):
            return bass.AP(t.tensor, row0 * 1024 + h * 128, dims)

        def head(slots, h):
            sA, sB = slots
            wa, wb_ = self.WS[sA], self.WS[sB]
            pending = [None]
            for i, tok0, rows in self.tiles():
                b = i % 2
                pb = self.bank(b)
                rb = [self.r_bank[b]]
                for kc in range(NKC):
                    sc.op("pe", lambda e, kc=kc: e.matmul(pb[0:rows, 0:256], lhsT=self.XNT[:, kc, tok0:tok0 + rows], rhs=wa[:, kc, 0:256],
                                                        start=(kc == 0), stop=(kc == NKC - 1)),
                          [self.r_xnt, self.r_ws[sA]], rb, signal=False)
                for kc in range(NKC):
                    sc.op("pe", lambda e, kc=kc: e.matmul(pb[0:rows, 256:384], lhsT=self.XNT[:, kc, tok0:tok0 + rows], rhs=wb_[:, kc, 0:128],
                                                        start=(kc == 0), stop=(kc == NKC - 1)),
                          [self.r_xnt, self.r_ws[sB]], rb, signal=(kc == NKC - 1))
                qf, rqf = self.QF[i % 2], self.r_qf[i % 2]
                qb, rqb = self.QB[i % 2], self.r_qb[i % 2]
                if pending[0] is not None:
                    pending[0]()
                    pending[0] = None
                self.norm_rope(pb[0:rows, 0:384], rows, 3, self.GQ[l], i, qf[0:rows, :, :], rqf, rb)
                sc.op("dve", lambda e: e.tensor_copy(out=qb[0:rows, :, :], in_=qf[0:rows, :, :]), [rqf], [rqb])

                def later(i=i, tok0=tok0, rows=rows, qb=qb, rqb=rqb):
                    for g in range(3):
                        sc.op("pe", lambda e, g=g: e.transpose(out=psT2[:, g * 128:g * 128 + rows], in_=qb[0:rows, g, :],
                                                              identity=self.identb[0:rows, 0:rows]), [rqb, self.r_const], rb2)
                    qtn, rqtn = self.QTN[i % 2], self.r_qtn[i % 2]
                    sc.op("act", lambda e: e.activation(out=qtn[:, 0:384], in_=psT2[:, 0:384], func=AF.Copy), rb2, [rqtn])
                    if i < 8:
                        sc.op("pool", lambda e: e.tensor_copy(out=self.QT[:, 0, tok0:tok0 + 128], in_=qtn[:, 0:128]), [rqtn], [self.r_qt[0]])
                        d1 = self.QT[:, 1, 0:1024].rearrange("c (r j) -> c r j", r=4)[:, :, i * 32:(i + 1) * 32]
                        sc.op("pool", lambda e: e.tensor_copy(out=d1, in_=qtn[:, 128:256].rearrange("c (m r) -> c r m", r=4)),
                              [rqtn], [self.r_qt[1]])
                        d2 = self.QT[:, 2, 0:1024].rearrange("c (r j) -> c r j", r=16)[:, :, i * 8:(i + 1) * 8]
                        sc.op("pool", lambda e: e.tensor_copy(out=d2, in_=qtn[:, 256:384].rearrange("c (m r) -> c r m", r=16)),
                              [rqtn], [self.r_qt[2]])
                    else:
                        sc.op("pool", lambda e: e.tensor_copy(out=self.QS[:, 0:12].rearrange("c (g s) -> c g s", g=3),
                                                              in_=qtn[:, 0:384].rearrange("c (g m) -> c g m", g=3)[:, :, 0:4]),
                              [rqtn], [self.r_qs])

                pending[0] = later
            if pending[0] is not None:
                pending[0]()

            if KDBG & 16:
                return

            def load_tiles(g, which):
                for (dst, rdst, exd_, exs_, rxd, rxs, nm) in ((self.KC, self.r_kc, self.exd_k, self.exs_k, self.r_exd_k, self.r_exs_k, "k"),
                                                              (self.VC, self.r_vc, self.exd_v, self.exs_v, self.r_exd_v, self.r_exs_v, "v")):
                    if nm != which:
                        continue
                    pairs = []
                    r0 = 0
                    if g == 0:
                        pairs.append((dst[:, 0, :], kv_ap(exd_, r0 + 896, h, [[1024, 128], [1, 128]])))
                        pairs.append((dst[:, 1:9, :], kv_ap(exs_, r0, h, [[1024, 128], [128 * 1024, 8], [1, 128]])))
                    elif g == 1:
                        pairs.append((dst[:, 0:12:3, :], kv_ap(exd_, r0 + 512, h, [[4096, 128], [1024, 4], [1, 128]])))
                        pairs.append((dst[:, 1:12:3, :], kv_ap(exs_, r0, h, [[4096, 128], [1024, 4], [1, 128]])))
                        pairs.append((dst[:, 2:12:3, :], kv_ap(exs_, r0 + 512, h, [[4096, 128], [1024, 4], [1, 128]])))
                    else:
                        pairs.append((dst[0:64, 0:16, :], kv_ap(exd_, r0, h, [[16384, 64], [1024, 16], [1, 128]])))
                        pairs.append((dst[64:128, 0:16, :], kv_ap(exs_, r0, h, [[16384, 64], [1024, 16], [1, 128]])))
                    sc.dma("sp", pairs, [rxd, rxs], [rdst])
                return (9, 12, 16)[g]

            def load_sample_k():
                sc.dma("pool", [(self.KC[:, 0:16, :], bass.AP(self.ck.tensor, l * 2048 * 1024 + h * 128, [[1024, 128], [128 * 1024, 16], [1, 128]]))],
                       [], [self.r_kc])

            def transpose_k(ntile):
                t0 = 0
                while t0 < ntile:
                    nb = min(8, ntile - t0)
                    for k in range(nb):
                        sc.op("pe", lambda e, k=k: e.transpose(out=psT2[:, k * 128:(k + 1) * 128], in_=self.KC[:, t0 + k, :],
                                                              identity=self.identb[:, :]), [self.r_kc, self.r_const], rb2)
                    sc.op("act", lambda e: e.activation(out=self.KT[:, t0:t0 + nb, :],
                                                        in_=psT2[:, 0:nb * 128].rearrange("c (t k) -> c t k", t=nb), func=AF.Copy),
                          rb2, [self.r_kt])
                    t0 += nb

            O, rO = self.bank(5), [self.r_bank[5]]
            Dn, rD = self.bank(6), [self.r_bank[6]]
            nS = {"n": 0}

            def evac(g, hf):
                if g == 0:
                    nv = self.NUM[:, hf * 512:(hf + 1) * 512]
                    dv = self.DEN[:, hf * 512:(hf + 1) * 512]
                    ov, dnv = O[:, 0:512], Dn[:, 0:512]
                    sc.op("act", lambda e: e.activation(out=nv, in_=ov, func=AF.Copy), rO, [self.r_num])
                    sc.op("act", lambda e: e.activation(out=dv, in_=dnv, func=AF.Copy), rD, [self.r_den])
                    return
                if g == 1:
                    nv = self.NUM[:, 0:1024].rearrange("c (j r) -> c r j", r=4)[:, 2 * hf:2 * hf + 2, :]
                    dv = self.DEN[:, 0:1024].rearrange("c (j r) -> c r j", r=4)[:, 2 * hf:2 * hf + 2, :]
                    ov = O[:, 0:512].rearrange("c (r j) -> c r j", r=2)
                    dnv = Dn[:, 0:512].rearrange("c (r j) -> c r j", r=2)
                else:
                    nv = self.NUM[:, 0:1024].rearrange("c (j r) -> c r j", r=16)[:, 8 * hf:8 * hf + 8, :]
                    dv = self.DEN[:, 0:1024].rearrange("c (j r) -> c r j", r=16)[:, 8 * hf:8 * hf + 8, :]
                    ov = O[:, 0:512].rearrange("c (r j) -> c r j", r=8)
                    dnv = Dn[:, 0:512].rearrange("c (r j) -> c r j", r=8)
                sc.op("dve", lambda e: e.tensor_tensor(out=nv, in0=ov, in1=nv, op=ALU.add), rO, [self.r_num])
                sc.op("dve", lambda e: e.tensor_tensor(out=dv, in0=dnv, in1=dv, op=ALU.add), rD, [self.r_den])

            def job_parts(g, ip, isame, qcols, mask, slot, n):
                Sb = self.bank(3 + n % 2)
                rS = [self.r_bank[3 + n % 2]]
                pt, rpt = self.PT[n % 2], self.r_pt[n % 2]
                oc = O[:, slot * 128:(slot + 1) * 128]
                dc = Dn[:, slot * 128:(slot + 1) * 128]

                def partA():
                    sc.op("pe", lambda e: e.matmul(Sb[:, 0:128], lhsT=self.KT[:, ip, :], rhs=qcols, start=True, stop=True),
                          [self.r_kt, self.r_qt[g]], rS, signal=False)
                    sc.op("pe", lambda e: e.matmul(Sb[:, 128:256], lhsT=self.KT[:, isame, :], rhs=qcols, start=True, stop=True),
                          [self.r_kt, self.r_qt[g]], rS)
                    sc.op("act", lambda e: e.activation(out=pt[:, :], in_=Sb[:, 0:256], func=AF.Exp, scale=SCALE), rS, [rpt])
                    sc.op("dve", lambda e: e.tensor_tensor(out=pt[:, :], in0=pt[:, :], in1=mask[:, :], op=ALU.mult), [self.r_const], [rpt])

                def partB():
                    sc.op("pe", lambda e: e.matmul(oc, lhsT=self.VC[:, ip, :], rhs=pt[:, 0:128], start=True, stop=False),
                          [self.r_vc, rpt], rO, signal=False)
                    sc.op("pe", lambda e: e.matmul(oc, lhsT=self.VC[:, isame, :], rhs=pt[:, 128:256], start=False, stop=True),
                          [self.r_vc, rpt], rO, signal=False)
                    sc.op("pe", lambda e: e.matmul(dc, lhsT=self.onesb[:, :], rhs=pt[:, 0:128], start=True, stop=False),
                          [self.r_const, rpt], rD, signal=False)
                    sc.op("pe", lambda e: e.matmul(dc, lhsT=self.onesb[:, :], rhs=pt[:, 128:256], start=False, stop=True),
                          [self.r_const, rpt], rD)
                return partA, partB

            def run_pipe(parts):
                prevB = None
                for (pa, pb_) in parts:
                    pa()
                    if prevB is not None:
                        prevB()
                    prevB = pb_
                if prevB is not None:
                    prevB()

            nt = load_tiles(0, "k")
            load_tiles(0, "v")
            transpose_k(nt)
            load_tiles(1, "k")
            parts = []
            for qt in range(8):
                pa, pb_ = job_parts(0, qt, qt + 1, self.QT[:, 0, qt * 128:(qt + 1) * 128], self.M1F if qt == 0 else self.M1, qt % 4, qt)
                if qt % 4 == 3:
                    pb_ = (lambda pb_=pb_, hf=qt // 4: (pb_(), evac(0, hf)))
                parts.append((pa, pb_))
            run_pipe(parts)
            if KDBG & 32:
                return
            nt = 12
            load_tiles(1, "v")
            transpose_k(nt)
            load_tiles(2, "k")
            parts = []
            k = 0
            for r in range(4):
                for j in range(2):
                    pa, pb_ = job_parts(1, r * 3 + j, r * 3 + j + 1, self.QT[:, 1, r * 256 + j * 128:r * 256 + (j + 1) * 128],
                                        self.M1F if j == 0 else self.M1, k % 4, k)
                    if k % 4 == 3:
                        pb_ = (lambda pb_=pb_, hf=k // 4: (pb_(), evac(1, hf)))
                    parts.append((pa, pb_))
                    k += 1
            run_pipe(parts)
            nt = 16
            load_tiles(2, "v")
            transpose_k(nt)
            load_sample_k()
            parts = []
            for r4 in range(4):
                def mk(r4=r4):
                    n = r4
                    Sb = self.bank(3 + n % 2)
                    rS = [self.r_bank[3 + n % 2]]
                    pt, rpt = self.PT[n % 2], self.r_pt[n % 2]

                    def partA():
                        for k4 in range(4):
                            r = r4 * 4 + k4
                            sc.op("pe", lambda e, r=r, k4=k4: e.matmul(Sb[:, k4 * 64:(k4 + 1) * 64], lhsT=self.KT[:, r, :],
                                                                     rhs=self.QT[:, 2, r * 64:(r + 1) * 64], start=True, stop=True),
                                  [self.r_kt, self.r_qt[2]], rS, signal=(k4 == 3))
                        sc.op("act", lambda e: e.activation(out=pt[:, :], in_=Sb[:, 0:256], func=AF.Exp, scale=SCALE), rS, [rpt])
                        ptv = pt[:, :].rearrange("k (a q) -> k a q", a=4)
                        sc.op("dve", lambda e: e.tensor_tensor(out=ptv, in0=ptv, in1=mkap(self.M16[:, 0:1], [[0, 4], [1, 64]]), op=ALU.mult),
                              [self.r_const], [rpt])

                    def partB():
                        for k4 in range(4):
                            r = r4 * 4 + k4
                            sl = r % 8
                            sc.op("pe", lambda e, r=r, k4=k4, sl=sl: e.matmul(O[:, sl * 64:(sl + 1) * 64], lhsT=self.VC[:, r, :],
                                                                            rhs=pt[:, k4 * 64:(k4 + 1) * 64], start=True, stop=True),
                                  [self.r_vc, rpt], rO, signal=False)
                            sc.op("pe", lambda e, k4=k4, sl=sl: e.matmul(Dn[:, sl * 64:(sl + 1) * 64], lhsT=self.onesb[:, :],
                                                                      rhs=pt[:, k4 * 64:(k4 + 1) * 64], start=True, stop=True),
                                  [self.r_const, rpt], rD, signal=(k4 == 3))
                        if r4 % 2 == 1:
                            evac(2, r4 // 2)
                    return partA, partB
                parts.append(mk())
            run_pipe(parts)

            if KDBG & 64:
                return
            sc.dma("pool", [(self.VC[:, 0:16, :], bass.AP(self.cv.tensor, l * 2048 * 1024 + h * 128, [[1024, 128], [128 * 1024, 16], [1, 128]]))],
                   [], [self.r_vc])
            transpose_k(16)
            sc.op("pe", lambda e: e.transpose(out=psT2[:, 0:4], in_=self.KSB[0:4, h * 128:(h + 1) * 128], identity=self.identb[0:4, 0:4]),
                  [self.r_ksb, self.r_const], rb2)
            sc.op("act", lambda e: e.activation(out=self.KTN[:, 0:4], in_=psT2[:, 0:4], func=AF.Copy), rb2, [self.r_ktn])
            b7, r7 = self.bank(7), [self.r_bank[7]]
            for t in range(16):
                sc.op("pe", lambda e, t=t: e.matmul(b7[:, t * 12:(t + 1) * 12], lhsT=self.KT[:, t, :], rhs=self.QS[:, 0:12], start=True, stop=True),
                      [self.r_kt, self.r_qs], r7, signal=False)
            sc.op("pe", lambda e: e.matmul(b7[0:4, 192:204], lhsT=self.KTN[:, 0:4], rhs=self.QS[:, 0:12], start=True, stop=True),
                  [self.r_ktn, self.r_qs], r7)
            sc.op("act", lambda e: e.activation(out=self.PTS[:, 0:16, :], in_=b7[:, 0:192].rearrange("k (t q) -> k t q", t=16),
                                                func=AF.Exp, scale=SCALE), r7, [self.r_pts])
            sc.op("act", lambda e: e.activation(out=self.PTS[0:4, 16, :], in_=b7[0:4, 192:204], func=AF.Exp, scale=SCALE), r7, [self.r_pts])
            sc.op("dve", lambda e: e.tensor_tensor(out=self.PTS[:, 0:16, :], in0=self.PTS[:, 0:16, :], in1=self.MS[:, 0:16, :], op=ALU.mult),
                  [self.r_const], [self.r_pts])
            sc.op("dve", lambda e: e.tensor_tensor(out=self.PTS[0:4, 16, :], in0=self.PTS[0:4, 16, :], in1=self.MS[0:4, 16, :], op=ALU.mult),
                  [self.r_const], [self.r_pts])
            for t in range(16):
                sc.op("pe", lambda e, t=t: e.matmul(O[:, 0:12], lhsT=self.VC[:, t, :], rhs=self.PTS[:, t, :], start=(t == 0), stop=False),
                      [self.r_vc, self.r_pts], rO, signal=False)
            sc.op("pe", lambda e: e.matmul(O[:, 0:12], lhsT=self.VSB[0:4, h * 128:(h + 1) * 128], rhs=self.PTS[0:4, 16, :], start=False, stop=True),
                  [self.r_vsb, self.r_pts], rO)
            for t in range(16):
                sc.op("pe", lambda e, t=t: e.matmul(Dn[:, 0:12], lhsT=self.onesb[:, :], rhs=self.PTS[:, t, :], start=(t == 0), stop=False),
                      [self.r_const, self.r_pts], rD, signal=False)
            sc.op("pe", lambda e: e.matmul(Dn[:, 0:12], lhsT=self.onesb[0:4, :], rhs=self.PTS[0:4, 16, :], start=False, stop=True),
                  [self.r_const, self.r_pts], rD)
            sc.op("act", lambda e: e.activation(out=self.OSS[:, 0, :], in_=O[:, 0:12], func=AF.Copy), rO, [self.r_oss])
            sc.op("act", lambda e: e.activation(out=self.OSS[:, 1, :], in_=Dn[:, 0:12], func=AF.Copy), rD, [self.r_oss])
            for (k_, dstt, rd) in ((0, self.NUM, self.r_num), (1, self.DEN, self.r_den)):
                sc.op("dve", lambda e: e.tensor_tensor(out=dstt[:, 1024:1028], in0=self.OSS[:, k_, 0:4], in1=self.OSS[:, k_, 4:8], op=ALU.add),
                      [self.r_oss], [rd])
                sc.op("dve", lambda e: e.tensor_tensor(out=dstt[:, 1024:1028], in0=dstt[:, 1024:1028], in1=self.OSS[:, k_, 8:12], op=ALU.add),
                      [self.r_oss], [rd])

            self.fm_matmuls(0, sB, 1, NKC, self.xnt_rhs)
            u0, rb0 = self.uview(0)
            sz, rsz = self.TF[2], self.r_tf[2]
            sc.op("act", lambda e: e.activation(out=self.v3(sz[:, 0:TT]), in_=u0, func=AF.Silu), rb0, [rsz])
            sc.op("dve", lambda e: e.reciprocal(out=self.DEN[:, :], in_=self.DEN[:, :]), [], [self.r_den])
            sc.op("dve", lambda e: e.tensor_tensor(out=self.NUM[:, :], in0=self.NUM[:, :], in1=self.DEN[:, :], op=ALU.mult),
                  [self.r_den], [self.r_num])
            sc.op("dve", lambda e: e.tensor_tensor(out=self.YB[:, h, :], in0=self.NUM[:, :], in1=sz[:, 0:TT], op=ALU.mult),
                  [self.r_num, rsz], [self.r_yb[h]])
            sc.op("dve", lambda e: e.memset(self.NUM[:, 1028:1032], 1.0), [], [self.r_num])
            sc.op("dve", lambda e: e.memset(self.DEN[:, 1028:1032], 1.0), [], [self.r_den])

        for h in range(8 if not (KDBG & 128) else 1):
            ldA = (lambda h=h: self.load_slab([(self.wi(l, O_Q + h * 128, 128), 0), (self.wi(l, O_Q + 1024 + h * 128, 128), 128)], NKC))
            ldB = (lambda h=h: self.load_slab([(self.wi(l, O_Q + 2048 + h * 128, 128), 0), (self.wi(l, O_ZB + h * 128, 128), 128)], NKC))
            items.append(([ldA, ldB], (lambda slots, h=h: head(slots, h))))
        self.run_items(items)

    def phase3(self, l):
        sc = self.sc
        self.tphase("p2")
        self.sc.alias(self.r_merged, self.attn_res)
        ystore = ((self.UA, self.r_ua, HA), (self.YB, self.r_yb, 0), (self.UC, self.r_uc, HC), (self.UD, self.r_ud, HD))
        items = []
        for dp in range(8):
            for i in range(4):
                ldg = (lambda dp=dp, i=i: self.load_slab([(self.wi(l, O_GT + i * 2048 + dp * 256, 256), 0)], NKC))
                ldp = (lambda dp=dp, i=i: self.load_slab([(self.w_br[l, i, :, dp * 256:(dp + 1) * 256], 0)], 8))

                def fn(slots, dp=dp, i=i):
                    sg, sp_ = slots
                    Y, rY, H = ystore[i]
                    for hh in range(2):
                        dc = dp * 2 + hh
                        self.fm_matmuls(0, sg, hh, NKC, self.xnt_rhs)
                        self.fm_matmuls(1, sp_, hh, 8, lambda kc, gi: (Y[:, kc, H + gi * GW:H + (gi + 1) * GW], [rY[kc]]))
                        u0, rb0 = self.uview(0)
                        u1, rb1 = self.uview(1)
                        gs, rgs = self.TB[hh], self.r_tb[hh]
                        acc, racc = self.TF[hh], self.r_tf[hh]
                        tmp, rtmp = self.TF[2 + hh], self.r_tf[2 + hh]
                        sc.op("act", lambda e: e.activation(out=self.v3(gs[:, 0:TT]), in_=u0, func=AF.Sigmoid), rb0, [rgs])
                        if i == 0:
                            sc.op("dve", lambda e: e.tensor_tensor(out=self.v3(acc[:, 0:TT]), in0=u1, in1=self.v3(gs[:, 0:TT]), op=ALU.mult),
                                  rb1 + [rgs], [racc])
                        else:
                            sc.op("dve", lambda e: e.tensor_tensor(out=self.v3(tmp[:, 0:TT]), in0=u1, in1=self.v3(gs[:, 0:TT]), op=ALU.mult),
                                  rb1 + [rgs], [rtmp])
                            if i < 3:
                                sc.op("dve", lambda e: e.tensor_tensor(out=acc[:, 0:TT], in0=acc[:, 0:TT], in1=tmp[:, 0:TT], op=ALU.add),
                                      [rtmp], [racc])
                            else:
                                sc.op("dve", lambda e: e.tensor_tensor(out=self.MERGED[:, dc, :], in0=acc[:, 0:TT], in1=tmp[:, 0:TT], op=ALU.add),
                                      [rtmp, racc], [self.r_merged[dc]])
                items.append(([ldg, ldp], fn))
        self.run_items(items)

    def phase4(self, l):
        sc = self.sc
        self.tphase("p4")
        items = []
        cnt = {"n": 0}
        last = (l == NL - 1) or STAGE < 99
        for s_ in range(4):
            lda = (lambda s_=s_: self.load_slab([(self.w_out[l, :, s_ * 512:s_ * 512 + 256], 0)], NKC))
            ldb = (lambda s_=s_: self.load_slab([(self.w_out[l, :, s_ * 512 + 256:(s_ + 1) * 512], 0)], NKC))

            def fn(slots, s_=s_):
                c0 = s_ * 512
                for i, tok0, rows in self.tiles():
                    n = cnt["n"]
                    cnt["n"] += 1
                    b = n % 4
                    pb = self.bank(b)
                    rb = [self.r_bank[b]]
                    xsl, rx = self.XSL[n % 2], self.r_xsl[n % 2]
                    osl, ro = self.OSL[n % 2], self.r_osl[n % 2]
                    if l == 0:
                        src = self.xp[tok0:tok0 + rows, c0:c0 + 512] if i < 8 else self.xs[:, c0:c0 + 512]
                        rsrc = []
                    else:
                        src = self.x1[tok0:tok0 + rows, c0:c0 + 512]
                        rsrc = [self.r_x1]
                    sc.dma("act", [(xsl[0:rows, :], src)], rsrc, [rx])
                    for hf, slot in enumerate(slots):
                        ws = self.WS[slot]
                        for kc in range(NKC):
                            sc.op("pe", lambda e, kc=kc, ws=ws, hf=hf: e.matmul(
                                pb[0:rows, hf * 256:(hf + 1) * 256], lhsT=self.MERGED[:, kc, tok0:tok0 + rows], rhs=ws[:, kc, 0:256],
                                start=(kc == 0), stop=(kc == NKC - 1)),
                                [self.r_merged[kc], self.r_ws[slot]], rb, signal=(kc == NKC - 1 and hf == 1))
                    sc.op("dve", lambda e: e.tensor_tensor(out=osl[0:rows, :], in0=pb[0:rows, 0:512], in1=xsl[0:rows, :], op=ALU.add),
                          rb + [rx], [ro])
                    if last:
                        dst = self.yp[tok0:tok0 + rows, c0:c0 + 512] if i < 8 else self.ys[:, c0:c0 + 512]
                        sc.dma("sp", [(dst, osl[0:rows, :])], [ro], [self.out_res])
                    else:
                        sc.dma("sp", [(self.x1[tok0:tok0 + rows, c0:c0 + 512], osl[0:rows, :])], [ro], [self.r_x1])
            items.append(([lda, ldb], fn))
        self.run_items(items)

    def finish(self):
        sc = self.sc
        for sem in sc.dsem:
            if sem.val > 0:
                sc._wait("sp", (sem, sem.val))
        for k in ("pe", "act", "dve", "pool"):
            s = sc.sem[k]
            if s.val > 0:
                sc._wait("sp", (s, s.val))


def _rope_tables(base):
    half = 16
    inv = (np.float32(500000.0) ** (-(np.arange(half, dtype=np.float32) / np.float32(half)))).astype(np.float32)
    rc = np.zeros((128, 9, 32), np.float32)
    rs = np.zeros((128, 9, 32), np.float32)
    for i in range(9):
        if i < 8:
            pos = (base + i * 128 + np.arange(128)).astype(np.float32)
            n = 128
        else:
            pos = (PAST + np.arange(4)).astype(np.float32)
            n = 4
        ang = (pos[:, None] * inv[None, :]).astype(np.float32)
        c = np.cos(ang).astype(np.float32)
        s_ = np.sin(ang).astype(np.float32)
        rc[:n, i, 0:16] = c
        rc[:n, i, 16:32] = c
        rs[:n, i, 0:16] = -s_
        rs[:n, i, 16:32] = s_
    return rc, rs


def _masks(odd):
    bf = ml_dtypes.bfloat16
    k = np.arange(128)[:, None]
    q = np.arange(128)[None, :]
    m1 = np.zeros((128, 256), np.float32)
    m1[:, 0:128] = (q <= k)
    m1[:, 128:256] = (q >= k)
    m1f = m1.copy()
    if not odd:
        m1f[:, 0:128] = 0.0
    m16 = np.zeros((128, 64), np.float32)
    m16[0:64, :] = 1.0 if odd else 0.0
    kk = np.arange(64)[:, None]
    qq = np.arange(64)[None, :]
    m16[64:128, :] = (kk <= qq)
    ms = np.zeros((128, 17, 12), np.float32)
    for g, (w, d) in enumerate(((128, 1), (512, 4), (2048, 16))):
        for s_ in range(4):
            for tile in range(17):
                for p in range(128):
                    e = tile * 128 + p
                    if tile == 16 and p >= 4:
                        continue
                    dist = 2048 + s_ - e
                    if dist >= 0 and dist % d == 0 and dist // d <= w // d:
                        ms[p, tile, g * 4 + s_] = 1.0
    return m1.astype(bf), m1f.astype(bf), m16.astype(bf), ms.astype(bf)


def _params(inp, l):
    p = np.zeros((128, NPAR), np.float32)
    p[:, P_NG:P_NG + 16] = inp["norm_g"][l].reshape(16, 128).T
    aw = inp["a_conv_w"][l]
    for j in range(3):
        p[:, P_AW + j * 8:P_AW + j * 8 + 8] = aw[j].reshape(8, 128).T
    p[:, P_CS:P_CS + 8] = inp["c_scale"][l].reshape(8, 128).T
    dw = inp["d_conv_w"][l]
    for j in range(31):
        p[:, P_DW + j * 8:P_DW + j * 8 + 8] = dw[j].reshape(8, 128).T
    p[:, P_DB:P_DB + 8] = inp["d_conv_b"][l].reshape(8, 128).T
    p[:, P_LG:P_LG + 8] = inp["d_ln_g"][l].reshape(8, 128).T
    p[:, P_LB:P_LB + 8] = inp["d_ln_b"][l].reshape(8, 128).T
    return p


_PROG = None


def kernel(**inp):
    global _PROG
    inp = {k: np.asarray(v) for k, v in inp.items()}
    if _PROG is None:
        _PROG = Prog()
    prog = _PROG
    w_in = np.ascontiguousarray(inp["w_in"][:, :, KW0:KW1], dtype=np.float32)
    w_br = np.ascontiguousarray(np.stack([inp["w_br_a"], inp["w_br_b"], inp["w_br_c"], inp["w_br_d"]], axis=1))
    w_out = np.ascontiguousarray(inp["w_out"])
    w_pool = np.ascontiguousarray(inp["c_pool_w"])
    params = np.stack([_params(inp, l) for l in range(NL)], axis=0)
    gq = np.ascontiguousarray(np.broadcast_to(inp["q_norm_g"][:, None, :], (NL, 128, 128)))
    gk = np.ascontiguousarray(np.broadcast_to(inp["k_norm_g"][:, None, :], (NL, 128, 128)))
    in_maps = []
    for c in range(KCORES):
        b, half = c // 2, c % 2
        rc, rs = _rope_tables(half * T)
        m1, m1f, m16, ms = _masks(half == 1)
        invc = np.zeros((128, 4, 16), np.float32)
        for g, w in enumerate((2, 4, 8, 16)):
            pos = half * T + np.arange(16)
            invc[:, g, :] = (1.0 / np.minimum(pos + 1, w)).astype(np.float32)[None, :]
        in_maps.append({
            "xp": np.ascontiguousarray(inp["x_prompt"][b, half * T:(half + 1) * T]),
            "xs": np.ascontiguousarray(inp["x_sample"][c]),
            "ck": np.ascontiguousarray(inp["cache_attn_k"][:, c].reshape(NL, 2048, 1024)),
            "cv": np.ascontiguousarray(inp["cache_attn_v"][:, c].reshape(NL, 2048, 1024)),
            "sa": np.ascontiguousarray(inp["state_conv_a"][:, c]),
            "scs": np.ascontiguousarray(inp["state_pool_c"][:, c]),
            "sd": np.ascontiguousarray(inp["state_conv_d"][:, c]),
            "w_in": w_in, "w_br": w_br, "w_out": w_out, "w_pool": w_pool, "params": params, "gq": gq, "gk": gk,
            "ropec": rc, "ropes": rs, "m1": m1, "m1f": m1f, "m16": m16, "msk_s": ms,
            "flag": np.full((128, 1), float(half), np.float32), "invc": invc,
        })
    res = run_bass_kernel_spmd(prog.nc, in_maps, core_ids=list(range(KCORES)), **({'trace': True} if os.environ.get('KTRACE') else {}))
    global _RES
    _RES = res
    R = list(res.results)
    global _LAST
    _LAST = R[:KCORES]
    while len(R) < 8:
        R.append({k: np.zeros_like(np.asarray(v)) for k, v in R[0].items()})

    def g(c, name):
        return np.asarray(R[c][name])
    y_p = np.zeros((4, 2048, DM), np.float32)
    nk_p = np.zeros((NL, 4, 2048, 8, 128), np.float32)
    nv_p = np.zeros((NL, 4, 2048, 8, 128), np.float32)
    for c in range(8):
        b, half = c // 2, c % 2
        y_p[b, half * T:(half + 1) * T] = g(c, "yp")
        nk_p[:, b, half * T:(half + 1) * T] = g(c, "nkp").reshape(NL, T, 8, 128)
        nv_p[:, b, half * T:(half + 1) * T] = g(c, "nvp").reshape(NL, T, 8, 128)
    y_s = np.stack([g(c, "ys") for c in range(8)], axis=0)
    nca_p = np.stack([g(2 * b + 1, "ncap") for b in range(4)], axis=1)
    npc_p = np.stack([g(2 * b + 1, "npcp") for b in range(4)], axis=1)
    ncd_p = np.stack([g(2 * b + 1, "ncdp") for b in range(4)], axis=1)
    nk_s = np.stack([g(c, "nks").reshape(NL, 2048, 8, 128) for c in range(8)], axis=1)
    nv_s = np.stack([g(c, "nvs").reshape(NL, 2048, 8, 128) for c in range(8)], axis=1)
    nca_s = np.stack([g(c, "ncas") for c in range(8)], axis=1)
    npc_s = np.stack([g(c, "npcs") for c in range(8)], axis=1)
    ncd_s = np.stack([g(c, "ncds") for c in range(8)], axis=1)
    return (y_p, y_s, nk_p, nv_p, nca_p, npc_p, ncd_p, nk_s, nv_s, nca_s, npc_s, ncd_s)
```
